# Optimizing a Trainium2 kernel written in Bass

```python
import jax, jax.numpy as jnp
from jax import lax
import numpy as np

D_MODEL = 1024
BATCH = 8
SEQ = 4096
DEPTH = 2

CHUNK = 64
CONV_CH = 512
CONV_WIDTH = 31
HG_HEADS = 4
HG_DK = 128
HG_DV = 128
HG_WIDTH = HG_HEADS * HG_DK
SB_HEADS = 8
SB_DH = 64
SB_WIDTH = SB_HEADS * SB_DH
N_BRANCH = 3
D_FF = 4 * D_MODEL
QBLK = 128
EPS = 1e-6

IN_SIZES = (CONV_CH, CONV_CH, HG_WIDTH, HG_WIDTH, HG_HEADS * HG_DV, HG_HEADS * HG_DV,
            SB_WIDTH, SB_WIDTH, SB_WIDTH, N_BRANCH * D_MODEL)
D_IN = int(sum(IN_SIZES))
SPLIT_IDX = [int(v) for v in np.cumsum(IN_SIZES)[:-1]]

kernel_name = "hybrid_conv_hgrn2_stickbreak_block"


def rms_norm(x, g):
    xf = x.astype(jnp.float32)
    y = xf * lax.rsqrt(jnp.mean(xf * xf, axis=-1, keepdims=True) + EPS)
    return (y * g.astype(jnp.float32)).astype(x.dtype)


def layer_norm(x, g, b):
    xf = x.astype(jnp.float32)
    mu = jnp.mean(xf, axis=-1, keepdims=True)
    var = jnp.mean(jnp.square(xf - mu), axis=-1, keepdims=True)
    y = (xf - mu) * lax.rsqrt(var + EPS)
    return (y * g.astype(jnp.float32) + b.astype(jnp.float32)).astype(x.dtype)


def conv_branch(a, gate, w, b, ln_g, ln_b, w_proj):
    u = a * jax.nn.sigmoid(gate)
    u = lax.conv_general_dilated(
        u, w[:, None, :], window_strides=(1,), padding=((CONV_WIDTH - 1, 0),),
        dimension_numbers=('NWC', 'WIO', 'NWC'), feature_group_count=CONV_CH) + b
    u = jax.nn.silu(layer_norm(u, ln_g, ln_b))
    return u @ w_proj


def hgrn2_branch(q, f, i, g, lb, norm_g, w_proj):
    B, S, _ = q.shape
    n_chunks = S // CHUNK
    f32 = jnp.float32

    def heads(t, d):
        return t.reshape(B, n_chunks, CHUNK, HG_HEADS, d).transpose(1, 0, 3, 2, 4)

    k = (1.0 - lb.astype(f32)) * jax.nn.sigmoid(-f.astype(f32))
    log_f = jnp.log1p(-k)
    qh = heads(jax.nn.silu(q.astype(f32)), HG_DK)
    kh = heads(k, HG_DK)
    lfh = heads(log_f, HG_DK)
    vh = heads(i.astype(f32), HG_DV)
    causal = jnp.tril(jnp.ones((CHUNK, CHUNK), dtype=bool))[:, :, None]

    def step(state, inp):
        qc, kc, lc, vc = inp
        b = jnp.cumsum(lc, axis=2)
        o_inter = jnp.einsum('bhtk,bhkv->bhtv', qc * jnp.exp(b), state)
        rel = b[:, :, :, None, :] - b[:, :, None, :, :]
        decay = jnp.exp(jnp.where(causal, rel, -jnp.inf))
        scores = jnp.einsum('bhtsk,bhsk->bhts', qc[:, :, :, None, :] * decay, kc)
        o = o_inter + jnp.einsum('bhts,bhsv->bhtv', scores, vc)
        b_last = b[:, :, -1:, :]
        new_state = (jnp.exp(b_last[:, :, 0, :])[..., None] * state
                     + jnp.einsum('bhsk,bhsv->bhkv', kc * jnp.exp(b_last - b), vc))
        return new_state, o

    s0 = jnp.zeros((B, HG_HEADS, HG_DK, HG_DV), f32)
    _, o = lax.scan(step, s0, (qh, kh, lfh, vh))
    o = o.transpose(1, 0, 3, 2, 4).reshape(B, S, HG_HEADS, HG_DV)
    o = rms_norm(o, norm_g).reshape(B, S, HG_HEADS * HG_DV)
    o = (o * jax.nn.silu(g.astype(f32))).astype(q.dtype)
    return o @ w_proj


def stick_breaking_branch(q, k, v, qn_g, kn_g, w_proj):
    B, S, _ = q.shape
    qh = rms_norm(q.reshape(B, S, SB_HEADS, SB_DH), qn_g).transpose(0, 2, 1, 3)
    kh = rms_norm(k.reshape(B, S, SB_HEADS, SB_DH), kn_g).transpose(0, 2, 1, 3)
    vh = v.reshape(B, S, SB_HEADS, SB_DH).transpose(0, 2, 1, 3)
    scale = SB_DH ** -0.5
    outs = []
    for blk in range(S // QBLK):
        start, end = blk * QBLK, (blk + 1) * QBLK
        qb = qh[:, :, start:end]
        kb = kh[:, :, :end]
        vb = vh[:, :, :end]
        z = jnp.einsum('bhtd,bhsd->bhts', qb, kb).astype(jnp.float32) * scale
        t_pos = start + jnp.arange(QBLK)
        s_pos = jnp.arange(end)
        mask = s_pos[None, :] < t_pos[:, None]
        log_keep = jnp.where(mask, jax.nn.log_sigmoid(-z), 0.0)
        between = lax.cumsum(log_keep, axis=3, reverse=True) - log_keep
        a = jnp.where(mask, jnp.exp(jax.nn.log_sigmoid(z) + between), 0.0)
        outs.append(jnp.einsum('bhts,bhsd->bhtd', a.astype(vb.dtype), vb))
    o = jnp.concatenate(outs, axis=2).transpose(0, 2, 1, 3).reshape(B, S, SB_WIDTH)
    return o @ w_proj


def setup_inputs(seed: int = 0) -> dict:
    key = jax.random.key(seed)
    ks = jax.random.split(key, 24)

    def nrm(k, shape, scale):
        return jax.random.normal(k, shape, jnp.float32) * scale

    L, D = DEPTH, D_MODEL
    return {
        "x": nrm(ks[0], (BATCH, SEQ, D), 1.0),
        "c": nrm(ks[1], (BATCH, D), 1.0),
        "mod_w": nrm(ks[2], (L, D, 6 * D), 0.5 * D ** -0.5),
        "mod_b": nrm(ks[3], (L, 6 * D), 0.01),
        "norm1_g": 1.0 + nrm(ks[4], (L, D), 0.02),
        "w_in": nrm(ks[5], (L, D, D_IN), D ** -0.5),
        "gate_b": nrm(ks[6], (L, N_BRANCH * D), 0.01),
        "conv_w": nrm(ks[7], (L, CONV_WIDTH, CONV_CH), CONV_WIDTH ** -0.5),
        "conv_b": nrm(ks[8], (L, CONV_CH), 0.01),
        "conv_ln_g": 1.0 + nrm(ks[9], (L, CONV_CH), 0.02),
        "conv_ln_b": nrm(ks[10], (L, CONV_CH), 0.01),
        "w_conv_proj": nrm(ks[11], (L, CONV_CH, D), CONV_CH ** -0.5),
        "hgrn_lb": nrm(ks[12], (L, HG_WIDTH), 0.5),
        "hgrn_norm_g": 1.0 + nrm(ks[13], (L, HG_DV), 0.02),
        "w_hgrn_proj": nrm(ks[14], (L, HG_HEADS * HG_DV, D), (HG_HEADS * HG_DV) ** -0.5),
        "sb_qn_g": 1.0 + nrm(ks[15], (L, SB_DH), 0.02),
        "sb_kn_g": 1.0 + nrm(ks[16], (L, SB_DH), 0.02),
        "w_sb_proj": nrm(ks[17], (L, SB_WIDTH, D), SB_WIDTH ** -0.5),
        "w_out": nrm(ks[18], (L, D, D), D ** -0.5),
        "norm2_g": 1.0 + nrm(ks[19], (L, D), 0.02),
        "mlp_w1": nrm(ks[20], (L, D, D_FF), D ** -0.5),
        "mlp_w2": nrm(ks[21], (L, D_FF, D), D_FF ** -0.5),
    }


def reference(x, c, mod_w, mod_b, norm1_g, w_in, gate_b, conv_w, conv_b, conv_ln_g,
              conv_ln_b, w_conv_proj, hgrn_lb, hgrn_norm_g, w_hgrn_proj, sb_qn_g,
              sb_kn_g, w_sb_proj, w_out, norm2_g, mlp_w1, mlp_w2):
    B, S, D = x.shape
    p = jax.nn.softmax(hgrn_lb.astype(jnp.float32), axis=0)
    lower_bounds = jnp.cumsum(p, axis=0) - p[0:1]
    c_act = jax.nn.silu(c)
    for l in range(DEPTH):
        mod = c_act @ mod_w[l] + mod_b[l]
        sh1, sc1, g1, sh2, sc2, g2 = [m[:, None, :] for m in jnp.split(mod, 6, axis=-1)]

        h = rms_norm(x, norm1_g[l]) * (1.0 + sc1) + sh1
        proj = h @ w_in[l]
        (cv_a, cv_g, hg_q, hg_f, hg_i, hg_g, sb_q, sb_k, sb_v, gl) = jnp.split(proj, SPLIT_IDX, axis=-1)
        y_conv = conv_branch(cv_a, cv_g, conv_w[l], conv_b[l], conv_ln_g[l], conv_ln_b[l], w_conv_proj[l])
        y_hgrn = hgrn2_branch(hg_q, hg_f, hg_i, hg_g, lower_bounds[l], hgrn_norm_g[l], w_hgrn_proj[l])
        y_sb = stick_breaking_branch(sb_q, sb_k, sb_v, sb_qn_g[l], sb_kn_g[l], w_sb_proj[l])
        gates = jax.nn.sigmoid(gl + gate_b[l]).reshape(B, S, N_BRANCH, D)
        merged = gates[:, :, 0] * y_conv + gates[:, :, 1] * y_hgrn + gates[:, :, 2] * y_sb
        x = x + g1 * (merged @ w_out[l])

        h2 = rms_norm(x, norm2_g[l]) * (1.0 + sc2) + sh2
        x = x + g2 * (jnp.square(jax.nn.relu(h2 @ mlp_w1[l])) @ mlp_w2[l])
    return x
```

```python
import numpy as np
import ml_dtypes
import concourse.bass as bass
import concourse.mybir as mybir
from concourse.bass_utils import run_bass_kernel_spmd

F32 = mybir.dt.float32
BF16 = mybir.dt.bfloat16
AF = mybir.ActivationFunctionType
ALU = mybir.AluOpType
NPBF = ml_dtypes.bfloat16

ENGS = ("pe", "act", "dve", "pool", "sp")
EPOCH = 24000


class Ent:
    __slots__ = ("w", "r")

    def __init__(self):
        self.w = None
        self.r = {}


class Op:
    __slots__ = ("fn", "deps", "dma", "sig", "slot", "seq")

    def __init__(self, fn, deps, dma, slot=0, seq=0):
        self.fn = fn
        self.deps = deps
        self.dma = dma
        self.sig = dma
        self.slot = slot
        self.seq = seq


NSLOT = 8


class Prog:
    def __init__(self, nc):
        self.nc = nc
        self.ops = {e: [] for e in ENGS}
        self.dmas = {e: [] for e in ENGS}
        self.sb_off = 16640
        self.sb_mark = []
        self.n_t = 0

    def sb(self, shape, dtype, name=None):
        esz = 2 if dtype == BF16 else 4
        n = 1
        for s in shape[1:]:
            n *= s
        nbytes = (n * esz + 63) // 64 * 64
        off = self.sb_off
        self.sb_off += nbytes
        assert self.sb_off <= 229376, f"SBUF overflow {self.sb_off}"
        self.n_t += 1
        return self.nc.alloc_sbuf_tensor_at(f"{name or 't'}{self.n_t}", list(shape), dtype, offset=off)

    def push(self):
        self.sb_mark.append(self.sb_off)

    def pop(self):
        self.sb_off = self.sb_mark.pop()

    def add(self, eng, fn, reads=(), writes=(), dma=False):
        idx = len(self.ops[eng])
        tok = (eng, idx)
        deps = set()
        for e in reads:
            if e.w is not None:
                deps.add(e.w)
        for e in writes:
            if e.w is not None:
                deps.add(e.w)
            for t in e.r.values():
                deps.add(t)
        deps.discard(tok)
        slot = seq = 0
        if dma:
            seq = len(self.dmas[eng])
            slot = seq % NSLOT
            if seq >= NSLOT:
                deps.add((eng, self.dmas[eng][seq - NSLOT]))
            self.dmas[eng].append(idx)
        rk = (eng, "d", slot) if dma else (eng, "c")
        for e in reads:
            e.r[rk] = tok
        for e in writes:
            e.w = tok
            e.r = {}
        self.ops[eng].append(Op(fn, deps, dma, slot, seq))
        return tok

    def barrier(self):
        last = []
        for e in ENGS:
            seen_c = False
            seen_d = set()
            for i in range(len(self.ops[e]) - 1, -1, -1):
                op = self.ops[e][i]
                if op.fn is None:
                    continue
                if op.dma:
                    if op.slot not in seen_d:
                        last.append((e, i))
                        seen_d.add(op.slot)
                elif not seen_c:
                    last.append((e, i))
                    seen_c = True
                if seen_c and len(seen_d) == NSLOT:
                    break
        for e in ENGS:
            self.ops[e].append(Op(None, set(last), False))
            self.ops[e][-1].sig = False

    def emit(self):
        nc = self.nc
        ops = self.ops
        for e in ENGS:
            for i, op in enumerate(ops[e]):
                for (e2, j) in op.deps:
                    if e2 == "pe" and e == "pe":
                        continue
                    ops[e2][j].sig = True
        tokval = {}
        sems = {}
        for e in ENGS:
            cnt = 0
            for i, op in enumerate(ops[e]):
                if op.fn is None or not op.sig:
                    continue
                if op.dma:
                    n = op.seq // NSLOT
                    per = EPOCH // 16
                    k = (e, "d", op.slot, n // per)
                    v = (n % per + 1) * 16
                else:
                    k = (e, "c", 0, cnt // EPOCH)
                    v = cnt % EPOCH + 1
                    cnt += 1
                tokval[(e, i)] = (k, v)
                if k not in sems:
                    sems[k] = nc.alloc_semaphore("s_" + "_".join(str(x) for x in k))
        self.n_sems = len(sems)

        def run(e, eng):
            waited = {}
            for i, op in enumerate(ops[e]):
                need = {}
                for (e2, j) in op.deps:
                    if e2 == "pe" and e == "pe":
                        continue
                    k, v = tokval[(e2, j)]
                    if need.get(k, 0) < v:
                        need[k] = v
                for k, v in need.items():
                    if waited.get(k, 0) >= v:
                        continue
                    if any(kk[:3] == k[:3] and kk[3] > k[3] for kk in waited):
                        continue
                    eng.wait_ge(sems[k], v)
                    waited[k] = v
                if op.fn is None:
                    continue
                ins = op.fn(eng)
                if op.sig:
                    k, v = tokval[(e, i)]
                    ins.then_inc(sems[k], 16 if op.dma else 1)

        with nc.Block() as block:
            @block.tensor
            def _(eng):
                run("pe", eng)

            @block.scalar
            def _(eng):
                run("act", eng)

            @block.vector
            def _(eng):
                run("dve", eng)

            @block.gpsimd
            def _(eng):
                run("pool", eng)

            @block.sync
            def _(eng):
                run("sp", eng)


D = 1024
KC = 8
TT = 512
C_CA, C_CG, C_HQ, C_HF, C_HI, C_HG, C_SQ, C_SK, C_SV, C_GL = 0, 512, 1024, 1536, 2048, 2560, 3072, 3584, 4096, 4608
NV = 240
CB_ID, CB_OD, CB_O512, CB_BD64, CB_O128, CB_NTRI, CB_NONE, CB_CMASK, CB_NMASK, NCB = 0, 128, 256, 384, 512, 640, 768, 896, 960, 3008
NCF = 516
USE_LO = False


def host_consts():
    cb = np.zeros((128, NCB), np.float32)
    p = np.arange(128)
    cb[:, CB_ID:CB_ID + 128] = np.eye(128)
    cb[:, CB_OD:CB_OD + 128] = 1.0 / 1024
    cb[:, CB_O512:CB_O512 + 128] = 1.0 / 512
    cb[:, CB_BD64:CB_BD64 + 128] = (p[:, None] // 64 == p[None, :] // 64) / 64.0
    cb[:, CB_O128:CB_O128 + 128] = 1.0 / 128
    cb[:, CB_NTRI:CB_NTRI + 128] = -(p[:, None] >= p[None, :]).astype(np.float32)
    cb[:, CB_NONE:CB_NONE + 128] = -1.0
    t = np.arange(512)
    for r in range(4):
        valid = (128 * r + p[:, None]) < t[None, :]
        cb[:, CB_NMASK + r * 512:CB_NMASK + (r + 1) * 512] = np.where(valid, 0.0, -30000.0)
    cb[:64, CB_CMASK:CB_CMASK + 64] = (p[:64, None] <= p[None, :64]).astype(np.float32)
    cb[64:, CB_CMASK:CB_CMASK + 64] = cb[:64, CB_CMASK:CB_CMASK + 64]
    cf = np.ones((128, NCF), np.float32)
    cf[:, 0:512:64] = 0.0
    cf[:, 512] = 1e-6
    cf[:, 513] = 1.0
    cf[:, 514] = 0.0
    return cb.astype(NPBF), cf


def host_vecs(inp, l):
    v = np.zeros((128, NV), np.float32)
    col = lambda a: np.asarray(a, np.float32).reshape(-1, 128).T
    v[:, 0:48] = col(inp["mod_b"][l])
    v[:, 48:56] = col(inp["norm1_g"][l])
    v[:, 56:64] = col(inp["norm2_g"][l])
    v[:, 64:88] = col(inp["gate_b"][l])
    v[:, 88:92] = col(inp["conv_b"][l])
    v[:, 92:96] = col(inp["conv_ln_g"][l])
    v[:, 96:100] = col(inp["conv_ln_b"][l])
    cw = np.asarray(inp["conv_w"][l], np.float32)
    v[:, 100:224] = cw.T.reshape(4, 128, 31).transpose(1, 0, 2).reshape(128, 124)
    v[:, 224:228] = col(inp["hgrn_lb"][0])
    v[:, 228:232] = col(inp["hgrn_lb"][1])
    v[:, 232] = np.asarray(inp["hgrn_norm_g"][l], np.float32)
    v[:, 233] = np.tile(np.asarray(inp["sb_qn_g"][l], np.float32), 2)
    v[:, 234] = np.tile(np.asarray(inp["sb_kn_g"][l], np.float32), 2)
    return v


class K:
    def __init__(self, P):
        self.P = P

    def mm(self, out, lhsT, rhs, start, stop, reads, writes, skip=False):
        self.P.add("pe", lambda g: g.matmul(out, lhsT=lhsT, rhs=rhs, start=start, stop=stop, skip_group_check=skip),
                   reads, writes)

    def act(self, out, in_, func, reads, writes, scale=None, bias=None):
        kw = {}
        if scale is not None:
            kw["scale"] = scale
        if bias is not None:
            kw["bias"] = bias
        self.P.add("act", lambda g: g.activation(out=out, in_=in_, func=func, **kw), reads, writes)

    def tt(self, eng, out, in0, in1, op, reads, writes):
        self.P.add(eng, lambda g: g.tensor_tensor(out=out, in0=in0, in1=in1, op=op), reads, writes)

    def ts(self, eng, out, in0, s1, op0, reads, writes, s2=None, op1=None):
        if op1 is None:
            self.P.add(eng, lambda g: g.tensor_scalar(out=out, in0=in0, scalar1=s1, scalar2=None, op0=op0), reads, writes)
        else:
            self.P.add(eng, lambda g: g.tensor_scalar(out=out, in0=in0, scalar1=s1, scalar2=s2, op0=op0, op1=op1),
                       reads, writes)

    def stt(self, out, in0, scalar, in1, op0, op1, reads, writes):
        self.P.add("dve", lambda g: g.scalar_tensor_tensor(out=out, in0=in0, scalar=scalar, in1=in1, op0=op0, op1=op1),
                   reads, writes)

    def copy(self, eng, out, in_, reads, writes):
        self.P.add(eng, lambda g: g.tensor_copy(out=out, in_=in_), reads, writes)

    def memset(self, eng, ap, val, writes):
        self.P.add(eng, lambda g: g.memset(ap, val), (), writes)

    def recip(self, out, in_, reads, writes):
        self.P.add("dve", lambda g: g.reciprocal(out=out, in_=in_), reads, writes)

    def dma(self, q, out, in_, reads, writes):
        self.P.add(q, lambda g: g.dma_start(out=out, in_=in_), reads, writes, dma=True)


def bcast_last(ap3, n):
    a = ap3.ap
    return bass.AP(ap3.tensor, ap3.offset, [list(a[0]), list(a[1]), [0, n]])


def build(S, L=2, dbg=()):
    NT = S // TT
    nc = bass.Bass("TRN2", target_bir_lowering=False)
    P = Prog(nc)
    k = K(P)

    def dram(name, shape, dt, kind="Internal"):
        return nc.dram_tensor(name, list(shape), dt, kind=kind).ap()

    xT_in = dram("xT", [NT, 128, 8, TT], F32, "ExternalInput")
    outT = dram("outT", [NT, 128, 8, TT], F32, "ExternalOutput")
    cc_d = dram("cc", [128, 8], F32, "ExternalInput")
    cb_d = dram("cbf", [128, NCB], BF16, "ExternalInput")
    cf_d = dram("cf32", [128, NCF], F32, "ExternalInput")
    vec_d = dram("vecs", [128, L, NV], F32, "ExternalInput")
    mod_w = dram("mod_w", [L, D, 6 * D], F32, "ExternalInput")
    w_in = dram("w_in", [L, D, 7680], F32, "ExternalInput")
    w_cp = dram("w_conv_proj", [L, 512, D], F32, "ExternalInput")
    w_hp = dram("w_hgrn_proj", [L, 512, D], F32, "ExternalInput")
    w_sp = dram("w_sb_proj", [L, 512, D], F32, "ExternalInput")
    w_out = dram("w_out", [L, D, D], F32, "ExternalInput")
    w1_d = dram("mlp_w1", [L, D, 4 * D], F32, "ExternalInput")
    w2_d = dram("mlp_w2", [L, 4 * D, D], F32, "ExternalInput")
    xm_s = dram("xm_s", [NT, 128, 8, TT], F32)
    x_s = dram("x_s", [NT, 128, 8, TT], F32)
    HT = dram("HT", [NT, 128, 8, TT], BF16)
    OC = dram("OC", [NT, 128, 4, TT], BF16)
    OH = dram("OH", [NT, 128, 4, TT], BF16)
    OS = dram("OS", [NT, 128, 4, TT], BF16)
    GP = dram("GP", [NT, 128, 24, TT], F32)
    e_GP = [Ent() for _ in range(NT)]
    e_xm, e_xs, e_OC, e_OH, e_OS, e_HT = Ent(), Ent(), Ent(), Ent(), Ent(), Ent()
    dbg_d = {n: dram("dbg_" + n, shp, F32, "ExternalOutput") for n, shp in dbg}

    psall = nc.alloc_psum_tensor("psall", [128, 8, 512], F32)
    ps = [psall[:, i, :] for i in range(8)]
    pe = [Ent() for _ in range(8)]
    rot = {"i": 0}

    def bank(lst=(0, 1, 2, 3, 4, 5, 6, 7)):
        rot["i"] += 1
        return lst[rot["i"] % len(lst)]

    def wview(w2d, c0, c1):
        return w2d.rearrange("(kc p) n -> p kc n", p=128)[:, :, c0:c1]

    def fview(t4d, t0, t1):
        assert t0 % TT == 0 and t1 == t0 + TT
        return t4d[t0 // TT]

    def fview_rm(t2d, t0, t1):
        return t2d.rearrange("(kc p) t -> p kc t", p=128)[:, :, t0:t1]

    def dbg_dump(name, src4d, e_src, kcn, bf):
        P.push()
        for tt_ in range(NT):
            d32 = P.sb([128, kcn, TT], F32); e_d32 = Ent()
            if bf:
                d16 = P.sb([128, kcn, TT], BF16); e_d16 = Ent()
                k.dma("sp", d16[:], src4d[tt_], [e_src], [e_d16])
                k.copy("dve", d32[:], d16[:], [e_d16], [e_d32])
            else:
                k.dma("sp", d32[:], src4d[tt_], [e_src], [e_d32])
            k.dma("sp", fview_rm(dbg_d[name], tt_ * TT, (tt_ + 1) * TT), d32[:], [e_d32], [])
        P.pop(); P.barrier()


    cb = P.sb([128, CB_NMASK], BF16); e_cb = Ent()
    cf = P.sb([128, NCF], F32); e_cf = Ent()
    vec = P.sb([128, L, NV], F32); e_vec = Ent()
    dv = P.sb([128, L, 80], F32); e_dv = Ent()
    cc = P.sb([128, 8], F32); e_cc = Ent()
    cact = P.sb([128, 8], BF16); e_cact = Ent()
    k.dma("sp", cb[:], cb_d[:, 0:CB_NMASK], [], [e_cb])
    k.dma("sp", cf[:], cf_d, [], [e_cf])
    k.dma("sp", vec[:], vec_d, [], [e_vec])
    k.dma("sp", cc[:], cc_d, [], [e_cc])
    k.act(cact[:], cc[:], AF.Silu, [e_cc], [e_cact])
    EPS = cf[:, 512:513]
    ONE = cf[:, 513:514]
    def mod_dma(l_, grp, mw, e_mw):
        b = grp % 2
        k.dma("pool", mw[b][:], wview(mod_w[l_], grp * 1024, (grp + 1) * 1024), [], [e_mw[b]])

    def mod_group(l_, grp, mw, e_mw, dma=True):
        b = grp % 2
        if dma:
            mod_dma(l_, grp, mw, e_mw)
        for j in range(8):
            col = grp * 8 + j
            for kc in range(8):
                k.mm(ps[7][:, col:col + 1], mw[b][:, kc, j * 128:(j + 1) * 128], cact[:, kc:kc + 1], kc == 0, kc == 7,
                     [e_mw[b], e_cact], [pe[7]])

    def mod_finish(l_):
        k.tt("dve", dv[:, l_, 0:48], ps[7][:, 0:48], vec[:, l_, 0:48], ALU.add, [e_vec], [pe[7], e_dv])
        k.stt(dv[:, l_, 48:56], dv[:, l_, 8:16], 1.0, vec[:, l_, 48:56], ALU.add, ALU.mult, [e_vec], [e_dv])
        k.stt(dv[:, l_, 56:64], dv[:, l_, 32:40], 1.0, vec[:, l_, 56:64], ALU.add, ALU.mult, [e_vec], [e_dv])
        k.ts("dve", dv[:, l_, 72:73], vec[:, l_, 233:234], 0.125, ALU.mult, [e_vec], [e_dv])

    P.push()
    mw0 = [P.sb([128, 8, 1024], BF16) for _ in range(2)]
    e_mw0 = [Ent(), Ent()]
    for grp in range(6):
        mod_group(0, grp, mw0, e_mw0)
    mod_finish(0)
    k.memset("dve", dv[:, 0, 64:68], 1.0, [e_dv])
    k.memset("dve", dv[:, 0, 68:72], -1.0, [e_dv])
    k.memset("dve", dv[:, 0, 76:80], 0.0, [e_dv])
    if L > 1:
        k.tt("dve", dv[:, 1, 76:80], vec[:, 0, 224:228], vec[:, 0, 228:232], ALU.subtract, [e_vec], [e_dv])
        k.act(dv[:, 1, 64:68], dv[:, 1, 76:80], AF.Sigmoid, [], [e_dv])
        k.ts("dve", dv[:, 1, 68:72], dv[:, 1, 64:68], -1.0, ALU.mult, [], [e_dv])
        k.ts("dve", dv[:, 1, 76:80], dv[:, 1, 64:68], -1.0, ALU.mult, [], [e_dv], s2=1.0, op1=ALU.add)
    P.pop()
    P.barrier()
    if "dv" in dbg_d:
        k.dma("sp", dbg_d["dv"], dv[:], [e_dv], [])

    def norm_mod(xt, e_xt, acol, shcol, dst_fn, e_dst, sq, e_sq, rs, e_rs, tmp, e_tmp):
        b = bank()
        if isinstance(sq, list):
            for kc in range(8):
                k.act(sq[kc % 2][:], xt[:, kc, :], AF.Square, [e_xt], [e_sq[kc % 2]])
                k.mm(ps[b][:], cb[:, CB_OD:CB_OD + 128], sq[kc % 2][:], kc == 0, kc == 7, [e_cb, e_sq[kc % 2]], [pe[b]])
        else:
            k.act(sq[:], xt[:], AF.Square, [e_xt], [e_sq])
            for kc in range(8):
                k.mm(ps[b][:], cb[:, CB_OD:CB_OD + 128], sq[:, kc, :], kc == 0, kc == 7, [e_cb, e_sq], [pe[b]])
        k.act(rs[:], ps[b][:], AF.Ln, [e_cf], [pe[b], e_rs], bias=EPS)
        k.act(rs[:], rs[:], AF.Exp, [], [e_rs], scale=-0.5)
        for kc in range(8):
            i2 = kc % 2
            k.stt(tmp[i2][:], xt[:, kc, :], acol(kc), rs[:], ALU.mult, ALU.mult, [e_xt, e_rs, e_dv], [e_tmp[i2]])
            if kc % 2 == 0:
                k.act(dst_fn(kc), tmp[i2][:], AF.Identity, [e_tmp[i2], e_dv], [e_dst], bias=shcol(kc))
            else:
                k.ts("dve", dst_fn(kc), tmp[i2][:], shcol(kc), ALU.add, [e_tmp[i2], e_dv], [e_dst])

    for l in range(L):
        x_src, e_xsrc = (xT_in, Ent()) if l == 0 else (x_s, e_xs)
        x_dst = outT if l == L - 1 else x_s
        P.push()
        hTt = [P.sb([128, 8, TT], BF16) for _ in range(2)]; e_hTt = [Ent(), Ent()]
        xt = [P.sb([128, 8, TT], F32) for _ in range(2)]; e_xt = [Ent(), Ent()]
        sq = P.sb([128, 8, TT], BF16); e_sq = Ent()
        rs = P.sb([128, TT], F32); e_rs = Ent()
        tmp = [P.sb([128, TT], F32) for _ in range(2)]; e_tmp = [Ent(), Ent()]
        k.dma("sp", xt[0][:], fview(x_src, 0, TT), [e_xsrc], [e_xt[0]])
        for tt in range(NT):
            b = tt % 2
            if tt + 1 < NT:
                k.dma("sp", xt[1 - b][:], fview(x_src, (tt + 1) * TT, (tt + 2) * TT), [e_xsrc], [e_xt[1 - b]])
            norm_mod(xt[b], e_xt[b], lambda kc: dv[:, l, 48 + kc:49 + kc], lambda kc: dv[:, l, kc:kc + 1],
                     lambda kc: hTt[b][:, kc, :], e_hTt[b], sq, e_sq, rs, e_rs, tmp, e_tmp)
            k.dma("sp", HT[tt], hTt[b][:], [e_hTt[b]], [e_HT])
        P.pop(); P.barrier()
        if f"hT{l}" in dbg_d:
            dbg_dump(f"hT{l}", HT, e_HT, 8, True)

        P.push()
        wa = P.sb([128, 8, 1024], BF16); e_wa = Ent()
        k.dma("pool", wa[:], wview(w_in[l], C_CA, C_CA + 1024), [], [e_wa])
        dg = P.sb([128, 4, 31, 128], BF16); e_dg2 = [Ent(), Ent()]
        uT = P.sb([128, 4, 30 + S], BF16); e_u = [Ent() for _ in range(NT)]
        e_upad = Ent()
        k.memset("pool", uT[:, :, 0:30], 0.0, [e_upad])
        dg_todo = [(c, j) for c in range(4) for j in range(31)]

        def dg_some(nops):
            for _ in range(nops):
                if dg_todo:
                    c_, j_ = dg_todo.pop(0)
                    k.ts("dve", dg[:, c_, j_, :], cb[:, CB_ID:CB_ID + 128],
                         vec[:, l, 100 + c_ * 31 + j_:101 + c_ * 31 + j_], ALU.mult, [e_cb, e_vec], [e_dg2[0]])
        sgt = [P.sb([128, TT], F32) for _ in range(2)]; e_sgt = [Ent(), Ent()]
        htl = [P.sb([128, 8, TT], BF16) for _ in range(2)]; e_htl = [Ent(), Ent()]
        k.dma("sp", htl[0][:], HT[0], [e_HT], [e_htl[0]])
        for tt in range(NT):
            if tt + 1 < NT:
                k.dma("sp", htl[(tt + 1) % 2][:], HT[tt + 1], [e_HT], [e_htl[(tt + 1) % 2]])
            for c in range(4):
                ba, bg = bank(), bank()
                for kc in range(8):
                    k.mm(ps[ba][:], wa[:, kc, c * 128:(c + 1) * 128], htl[tt % 2][:, kc, :], kc == 0, kc == 7,
                         [e_wa, e_htl[tt % 2]], [pe[ba]])
                for kc in range(8):
                    k.mm(ps[bg][:], wa[:, kc, 512 + c * 128:512 + (c + 1) * 128], htl[tt % 2][:, kc, :], kc == 0, kc == 7,
                         [e_wa, e_htl[tt % 2]], [pe[bg]])
                i2 = c % 2
                k.act(sgt[i2][:], ps[bg][:], AF.Sigmoid, [], [pe[bg], e_sgt[i2]])
                k.tt("dve", uT[:, c, 30 + tt * TT:30 + (tt + 1) * TT], ps[ba][:], sgt[i2][:], ALU.mult, [e_sgt[i2]], [pe[ba], e_u[tt]])
                dg_some((124 + 4 * NT - 1) // (4 * NT))
        dg_some(124)
        defer_mod = (l == 0 and L > 1)
        if defer_mod:
            mw1 = [P.sb([128, 8, 1024], BF16) for _ in range(2)]
            e_mw1 = [Ent(), Ent()]
            mod_dma(1, 0, mw1, e_mw1)
            mod_dma(1, 1, mw1, e_mw1)
        v32 = P.sb([128, 4, TT], F32); e_v32 = Ent()
        vbf = P.sb([128, 4, TT], BF16); e_vbf = Ent()
        vsq = P.sb([128, 4, TT], BF16); e_vsq = Ent()
        mu = P.sb([128, TT], F32); e_mu = Ent()
        var = P.sb([128, TT], F32); e_var = Ent()
        t1 = P.sb([128, 4, TT], F32); e_t1 = Ent()
        oc = [P.sb([128, 4, TT], BF16) for _ in range(2)]; e_oc = [Ent(), Ent()]
        for tt in range(NT):
            for c in range(4):
                b = bank((0, 1, 2, 3, 4, 5, 6))
                rd = [e_dg2[0], e_dg2[1], e_u[tt], e_upad] + ([e_u[tt - 1]] if tt > 0 else [])
                for j in range(31):
                    k.mm(ps[b][:], dg[:, c, j, :], uT[:, c, tt * TT + j:tt * TT + j + TT], j == 0, j == 30, rd, [pe[b]])
                k.act(v32[:, c, :], ps[b][:], AF.Identity, [e_vec], [pe[b], e_v32], bias=vec[:, l, 88 + c:89 + c])
            k.copy("pool", vbf[:], v32[:], [e_v32], [e_vbf])
            k.act(vsq[:], v32[:], AF.Square, [e_v32], [e_vsq])
            bm, bq = bank((0, 1, 2, 3, 4, 5, 6)), bank((0, 1, 2, 3, 4, 5, 6))
            for c in range(4):
                k.mm(ps[bm][:], cb[:, CB_O512:CB_O512 + 128], vbf[:, c, :], c == 0, c == 3, [e_cb, e_vbf], [pe[bm]])
            for c in range(4):
                k.mm(ps[bq][:], cb[:, CB_O512:CB_O512 + 128], vsq[:, c, :], c == 0, c == 3, [e_cb, e_vsq], [pe[bq]])
            k.act(mu[:], ps[bm][:], AF.Identity, [], [pe[bm], e_mu])
            k.stt(var[:], mu[:], -1.0, mu[:], ALU.mult, ALU.mult, [e_mu], [e_var])
            k.tt("dve", var[:], ps[bq][:], var[:], ALU.add, [], [pe[bq], e_var])
            k.ts("dve", var[:], var[:], 0.0, ALU.max, [], [e_var])
            k.act(var[:], var[:], AF.Ln, [e_cf], [e_var], bias=EPS)
            k.act(var[:], var[:], AF.Exp, [], [e_var], scale=-0.5)
            ob = tt % 2
            for c in range(4):
                k.tt("dve", t1[:, c, :], v32[:, c, :], mu[:], ALU.subtract, [e_v32, e_mu], [e_t1])
                k.tt("pool", t1[:, c, :], t1[:, c, :], var[:], ALU.mult, [e_var], [e_t1])
                k.act(oc[ob][:, c, :], t1[:, c, :], AF.Silu, [e_t1, e_vec], [e_oc[ob]],
                      scale=vec[:, l, 92 + c:93 + c], bias=vec[:, l, 96 + c:97 + c])
            k.dma("sp", fview(OC, tt * TT, (tt + 1) * TT), oc[ob][:], [e_oc[ob]], [e_OC])
            if defer_mod:
                grps = list(range(6 * tt // NT, 6 * (tt + 1) // NT))
                for gi, grp in enumerate(grps):
                    mod_group(1, grp, mw1, e_mw1, dma=False)
                    if grp + 2 < 6:
                        mod_dma(1, grp + 2, mw1, e_mw1)
        if defer_mod:
            mod_finish(1)
        P.pop(); P.barrier()
        if f"OC{l}" in dbg_d:
            dbg_dump(f"OC{l}", OC, e_OC, 4, True)
        P.push()
        QT = P.sb([128, 4, S], BF16); KT = P.sb([128, 4, S], BF16)
        VV = P.sb([128, S // 128, 512], BF16)
        e_Q = [[Ent() for _ in range(NT)] for _ in range(4)]
        e_K = [[Ent() for _ in range(NT)] for _ in range(4)]
        e_V = [Ent() for _ in range(S // 128)]
        P.push()
        wqk = P.sb([128, 8, 1024], BF16); e_wqk = Ent()
        wv = P.sb([128, 8, 512], BF16); e_wv = Ent()
        k.dma("pool", wqk[:], wview(w_in[l], C_SQ, C_SQ + 1024), [], [e_wqk])
        k.dma("pool", wv[:], wview(w_in[l], C_SV, C_SV + 512), [], [e_wv])
        sqb = [P.sb([128, TT], BF16) for _ in range(2)]; e_sqb = [Ent(), Ent()]
        q32 = [P.sb([128, TT], F32) for _ in range(2)]; e_q32 = [Ent(), Ent()]
        rsb = [P.sb([128, TT], F32) for _ in range(2)]; e_rsb = [Ent(), Ent()]
        it = 0
        htl = [P.sb([128, 8, TT], BF16) for _ in range(2)]; e_htl = [Ent(), Ent()]
        k.dma("sp", htl[0][:], HT[0], [e_HT], [e_htl[0]])
        for tt in range(NT):
            if tt + 1 < NT:
                k.dma("sp", htl[(tt + 1) % 2][:], HT[tt + 1], [e_HT], [e_htl[(tt + 1) % 2]])
            for which in range(2):
                for c in range(4):
                    i2 = it % 2; it += 1
                    b, b2 = bank(), bank()
                    for kc in range(8):
                        k.mm(ps[b][:], wqk[:, kc, which * 512 + c * 128:which * 512 + (c + 1) * 128],
                             htl[tt % 2][:, kc, :], kc == 0, kc == 7, [e_wqk, e_htl[tt % 2]], [pe[b]])
                    k.act(sqb[i2][:], ps[b][:], AF.Square, [], [pe[b], e_sqb[i2]])
                    k.copy("dve", q32[i2][:], ps[b][:], [], [pe[b], e_q32[i2]])
                    k.mm(ps[b2][:], cb[:, CB_BD64:CB_BD64 + 128], sqb[i2][:], True, True, [e_cb, e_sqb[i2]], [pe[b2]])
                    k.act(rsb[i2][:], ps[b2][:], AF.Ln, [e_cf], [pe[b2], e_rsb[i2]], bias=EPS)
                    k.act(rsb[i2][:], rsb[i2][:], AF.Exp, [], [e_rsb[i2]], scale=-0.5)
                    if which == 0:
                        k.stt(QT[:, c, tt * TT:(tt + 1) * TT], q32[i2][:], dv[:, l, 72:73], rsb[i2][:], ALU.mult, ALU.mult,
                              [e_q32[i2], e_rsb[i2], e_dv], [e_Q[c][tt]])
                    else:
                        k.stt(KT[:, c, tt * TT:(tt + 1) * TT], q32[i2][:], vec[:, l, 234:235], rsb[i2][:], ALU.mult, ALU.mult,
                              [e_q32[i2], e_rsb[i2], e_vec], [e_K[c][tt]])
            for tb in range(tt * 4, tt * 4 + 4):
                b = bank()
                for kc in range(8):
                    k.mm(ps[b][:], htl[tt % 2][:, kc, (tb % 4) * 128:(tb % 4 + 1) * 128], wv[:, kc, :], kc == 0, kc == 7, [e_wv, e_htl[tt % 2]], [pe[b]])
                if tb % 2:
                    k.act(VV[:, tb, :], ps[b][:], AF.Identity, [], [pe[b], e_V[tb]])
                else:
                    k.copy("dve", VV[:, tb, :], ps[b][:], [], [pe[b], e_V[tb]])
        P.pop(); P.barrier()
        nmk = P.sb([128, 2048], BF16); e_nmk = Ent()
        k.dma("sp", nmk[:], cb_d[:, CB_NMASK:CB_NMASK + 2048], [], [e_nmk])
        esb = P.sb([128, 2, TT], F32); e_esb = Ent()
        lkb = [P.sb([128, 2, TT], BF16) for _ in range(2)]; e_lk = [Ent(), Ent()]
        Ls = P.sb([128, 2, TT], F32); e_Ls = Ent()
        Lhi = [P.sb([128, 2, TT], BF16) for _ in range(2)]; e_Lhi = [Ent(), Ent()]
        ATb = [P.sb([128, 2, TT], BF16) for _ in range(2)]; e_AT = [Ent(), Ent()]
        osb = [P.sb([128, TT], BF16) for _ in range(2)]; e_osb = [Ent(), Ent()]
        OB = 6
        wgt = P.sb([128, 8, 3072], BF16); e_wgt = [Ent() for _ in range(3)]
        for q3 in range(3):
            k.dma("pool", wgt[:, :, q3 * 1024:(q3 + 1) * 1024], wview(w_in[l], C_GL + q3 * 1024, C_GL + (q3 + 1) * 1024), [], [e_wgt[q3]])
        hg = [P.sb([128, 8, TT], BF16) for _ in range(2)]; e_hg = [Ent(), Ent()]
        gpre = [P.sb([128, TT], F32) for _ in range(3)]; e_gpre = [Ent() for _ in range(3)]
        gstate = {"n": 0}

        gate_mms = [(tt_, ch_, kc_) for tt_ in range(NT) for ch_ in range(24) for kc_ in range(8)]

        def gate_mm():
            if gstate["n"] >= len(gate_mms):
                return
            n_ = gstate["n"]; gstate["n"] += 1
            tt_, ch_, kc = gate_mms[n_]
            hb = tt_ % 2
            if ch_ == 0 and kc == 0:
                k.dma("sp", hg[hb][:], HT[tt_], [e_HT], [e_hg[hb]])
            k.mm(ps[7][:], wgt[:, kc, ch_ * 128:(ch_ + 1) * 128], hg[hb][:, kc, :], kc == 0, kc == 7,
                 [e_wgt[ch_ // 8], e_hg[hb]], [pe[7]])
            if kc == 7:
                i3_ = (n_ // 8) % 3
                k.copy("dve", gpre[i3_][:], ps[7][:], [], [pe[7], e_gpre[i3_]])
                k.dma("sp", GP[tt_][:, ch_, :], gpre[i3_][:], [e_gpre[i3_]], [e_GP[tt_]])

        def make_group(g, c, base):
            nkb = 4 * (g + 1)
            kbs = list(range(nkb - 1, -1, -1))

            def c0of(i):
                r = kbs[i] - 4 * g
                return 128 * r if r > 0 else 0

            def Zmm(i):
                kb = kbs[i]; r = kb - 4 * g; j3 = (base + i) % 3; c0 = c0of(i)
                for hh in range(2):
                    pb = 64 * hh; zb = 2 * j3 + hh
                    k.mm(ps[zb][:, c0:], KT[pb:pb + 64, c, kb * 128:(kb + 1) * 128], QT[pb:pb + 64, c, g * TT + c0:(g + 1) * TT],
                         True, r < 0, [e_K[c][kb // 4], e_Q[c][g]], [pe[zb]])
                    if r >= 0:
                        k.mm(ps[zb][:, c0:], cb[:, CB_ID:CB_ID + 128], nmk[:, r * 512 + c0:(r + 1) * 512],
                             False, True, [e_cb, e_nmk], [pe[zb]])

            def A_act(i):
                j = (base + i) % 2; j3 = (base + i) % 3; c0 = c0of(i)
                k.act(esb[:, :, c0:], psall[:, 2 * j3:2 * j3 + 2, c0:], AF.Exp, [], [pe[2 * j3], pe[2 * j3 + 1], e_esb])
                k.act(lkb[j][:, :, c0:], esb[:, :, c0:], AF.Ln, [e_esb, e_cf], [e_lk[j]], bias=ONE)

            def tail(i):
                if i == 0:
                    k.memset("pool", Ls[:], 0.0, [e_Ls])
                if i >= nkb - 1:
                    return
                j = (base + i) % 2; jn = (base + i + 1) % 2; c0 = c0of(i); c0n = c0of(i + 1)
                if c0n == c0:
                    k.tt("dve", Lhi[jn][:, :, c0:], Ls[:, :, c0:], lkb[j][:, :, c0:], ALU.add, [e_Ls, e_lk[j]], [e_Lhi[jn]])
                    k.tt("pool", Ls[:, :, c0:], Ls[:, :, c0:], lkb[j][:, :, c0:], ALU.add, [e_lk[j]], [e_Ls])
                else:
                    k.tt("dve", Ls[:, :, c0:], Ls[:, :, c0:], lkb[j][:, :, c0:], ALU.add, [e_lk[j]], [e_Ls])
                    k.copy("dve", Lhi[jn][:, :, c0n:], Ls[:, :, c0n:], [e_Ls], [e_Lhi[jn]])

            def B_tri(i):
                j = (base + i) % 2; j3 = (base + i) % 3; c0 = c0of(i)
                for hh in range(2):
                    zb = 2 * j3 + hh
                    k.mm(ps[zb][:, c0:], cb[:, CB_NTRI:CB_NTRI + 128], lkb[j][:, hh, c0:], False, i == 0, [e_cb, e_lk[j]], [pe[zb]], skip=True)
                    if i > 0:
                        k.mm(ps[zb][:, c0:], cb[:, CB_NONE:CB_NONE + 128], Lhi[j][:, hh, c0:], False, True, [e_cb, e_Lhi[j]], [pe[zb]], skip=True)

            def B_exp(i):
                j = (base + i) % 2; j3 = (base + i) % 3; c0 = c0of(i)
                k.act(ATb[j][:, :, c0:], psall[:, 2 * j3:2 * j3 + 2, c0:], AF.Exp, [], [pe[2 * j3], pe[2 * j3 + 1], e_AT[j]])

            def AV(i):
                kb = kbs[i]; j = (base + i) % 2; c0 = c0of(i)
                for hh in range(2):
                    h = 2 * c + hh
                    k.mm(ps[OB][64 * hh:64 * hh + 64, c0:], VV[:, kb, h * 64:(h + 1) * 64], ATb[j][:, hh, c0:], i == 0, i == nkb - 1,
                         [e_V[kb], e_AT[j]], [pe[OB]], skip=True)
                if i == nkb - 1:
                    ob_ = (g * 4 + c) % 2
                    k.copy("dve", osb[ob_][:], ps[OB][:], [], [pe[OB], e_osb[ob_]])
                    k.dma("sp", OS[g, :, c, :], osb[ob_][:], [e_osb[ob_]], [e_OS])

            return dict(nkb=nkb, Zmm=Zmm, A_act=A_act, tail=tail, B_tri=B_tri, B_exp=B_exp, AV=AV)

        steps = []
        base = 0
        for g in range(NT):
            for c in range(4):
                G_ = make_group(g, c, base)
                for i in range(G_["nkb"]):
                    steps.append((G_, i))
                base += G_["nkb"]
        G0, i0 = steps[0]
        G0["Zmm"](i0); G0["A_act"](i0); G0["tail"](i0)
        gacc = 0.0
        gper = len(gate_mms) / float(len(steps))
        for kk_ in range(len(steps)):
            Gc, i = steps[kk_]
            nxt = steps[kk_ + 1] if kk_ + 1 < len(steps) else None
            if nxt:
                nxt[0]["Zmm"](nxt[1])
            Gc["B_tri"](i)
            if kk_ >= 1:
                Gp, ip = steps[kk_ - 1]
                Gp["AV"](ip)
            gacc += gper
            while gacc >= 1.0 - 1e-9:
                gate_mm()
                gacc -= 1.0
            if nxt:
                nxt[0]["A_act"](nxt[1])
            Gc["B_exp"](i)
            if nxt:
                nxt[0]["tail"](nxt[1])
        Gl, il = steps[-1]
        Gl["AV"](il)
        while gstate["n"] < len(gate_mms):
            gate_mm()
        P.pop(); P.barrier()
        if f"OS{l}" in dbg_d:
            dbg_dump(f"OS{l}", OS, e_OS, 4, True)
        P.push()
        wh = P.sb([128, 8, 2048], BF16); e_wh = Ent()
        k.dma("pool", wh[:], wview(w_in[l], C_HQ, C_HQ + 2048), [], [e_wh])
        V64 = [P.sb([128, 4, 512], BF16) for _ in range(2)]; e_V64 = [Ent(), Ent()]
        qs = P.sb([128, 4, TT], F32); e_qs = Ent()
        sg = P.sb([128, 4, TT], F32); e_sg = Ent()
        kk = P.sb([128, 4, TT], F32); e_kk = Ent()
        lf = P.sb([128, 4, TT], F32); e_lf = Ent()
        bb = P.sb([128, 4, TT], F32); e_bb = Ent()
        bp = P.sb([128, 4, TT], F32); e_bp = Ent()
        eb = sg; e_eb = e_sg
        enb = lf; e_enb = e_lf
        qt = [P.sb([128, 4, TT], BF16) for _ in range(2)]; e_qt = [Ent(), Ent()]
        kt = P.sb([128, 4, TT], BF16); e_kt = Ent()
        gs = [P.sb([128, 4, TT], BF16) for _ in range(2)]; e_gs = [Ent(), Ent()]
        d1 = [P.sb([128, 4, 8], F32) for _ in range(2)]; d2 = [P.sb([128, 4, 8], F32) for _ in range(2)]
        d3 = [P.sb([128, 4, 8], F32) for _ in range(2)]; e_d = [Ent(), Ent()]
        st32 = P.sb([128, 4, 128], F32); e_st = Ent()
        scA = [P.sb([128, 8, 4, 64], BF16) for _ in range(2)]; e_scA = [[Ent() for _ in range(8)] for _ in range(2)]
        ktok = [P.sb([128, 4, 128], BF16) for _ in range(2)]; e_ktok = [Ent(), Ent()]
        tm2A = [P.sb([128, 8, 4, 128], F32) for _ in range(2)]; e_tm2A = [[Ent() for _ in range(8)] for _ in range(2)]
        stbfA = P.sb([128, 8, 4, 128], BF16); e_stbfA = [Ent() for _ in range(8)]
        oT = P.sb([128, 4, TT], F32); e_oT = Ent()
        ors = P.sb([128, TT], F32); e_ors = Ent()
        ot2 = bb; e_ot2 = e_bb
        oh = P.sb([128, 4, TT], BF16); e_oh = Ent()
        k.memset("pool", st32[:], 0.0, [e_st])
        k.memset("dve", ps[0][:], 0.0, [pe[0]])
        def cmask_bc(po):
            cm = cb[po:po + 64, CB_CMASK:CB_CMASK + 64]
            return bass.AP(cm.tensor, cm.offset, [list(cm.ap[0]), [0, 4], [1, 64]])

        def bc128(d, n):
            v = d[:, :, n:n + 1]
            return bass.AP(v.tensor, v.offset, [list(v.ap[0]), list(v.ap[1]), [0, 128]])

        htl = [P.sb([128, 8, TT], BF16) for _ in range(2)]; e_htl = [Ent(), Ent()]

        def P1(tt):
            p = tt % 2
            hb, e_hb = htl[p], e_htl[p]
            for hh in range(4):
                b = bank((4, 5, 6, 7))
                for kc in range(8):
                    k.mm(ps[b][:], wh[:, kc, hh * 128:(hh + 1) * 128], hb[:, kc, :], kc == 0, kc == 7, [e_wh, e_hb], [pe[b]])
                k.act(qs[:, hh, :], ps[b][:], AF.Silu, [], [pe[b], e_qs])
                b = bank((4, 5, 6, 7))
                for kc in range(8):
                    k.mm(ps[b][:], wh[:, kc, 512 + hh * 128:512 + (hh + 1) * 128], hb[:, kc, :], kc == 0, kc == 7, [e_wh, e_hb], [pe[b]])
                k.act(sg[:, hh, :], ps[b][:], AF.Exp, [], [pe[b], e_sg])
                k.act(lf[:, hh, :], sg[:, hh, :], AF.Ln, [e_sg, e_dv], [e_lf], bias=dv[:, l, 76 + hh:77 + hh])
                k.act(bp[:, hh, :], sg[:, hh, :], AF.Ln, [e_sg, e_cf], [e_bp], bias=ONE)
                k.act(sg[:, hh, :], bp[:, hh, :], AF.Exp, [e_bp], [e_sg], scale=-1.0)
                k.ts("dve", kk[:, hh, :], sg[:, hh, :], dv[:, l, 64 + hh:65 + hh], ALU.mult, [e_sg, e_dv], [e_kk])
                k.tt("pool", lf[:, hh, :], lf[:, hh, :], bp[:, hh, :], ALU.subtract, [e_bp], [e_lf])
                b = bank((4, 5, 6, 7))
                for kc in range(8):
                    k.mm(ps[b][:], wh[:, kc, 1536 + hh * 128:1536 + (hh + 1) * 128], hb[:, kc, :], kc == 0, kc == 7, [e_wh, e_hb], [pe[b]])
                k.act(gs[p][:, hh, :], ps[b][:], AF.Silu, [], [pe[b], e_gs[p]])
            for blk in range(4):
                b = bank((4, 5, 6, 7))
                for kc in range(8):
                    k.mm(ps[b][:], hb[:, kc, blk * 128:(blk + 1) * 128], wh[:, kc, 1024:1536], kc == 0, kc == 7, [e_wh, e_hb], [pe[b]])
                k.copy("dve", V64[p][:, blk, :], ps[b][:], [], [pe[b], e_V64[p]])
            for hh in range(4):
                P.add("dve", (lambda hh: lambda g_: g_.tensor_tensor_scan(out=bb[:, hh, :], data0=cf[:, 0:512], data1=lf[:, hh, :],
                                                                        initial=0.0, op0=ALU.mult, op1=ALU.add))(hh),
                      [e_cf, e_lf], [e_bb])
            bbv = bb[:].rearrange("p h (n c) -> p (h n) c", c=64)
            bpv = bp[:].rearrange("p h (n c) -> p (h n) c", c=64)
            k.tt("dve", bpv, bbv, bcast_last(bbv[:, :, 31:32], 64), ALU.subtract, [e_bb], [e_bp])
            k.act(d1[p][:].rearrange("p h n -> p (h n)"), bbv[:, :, 63], AF.Exp, [e_bb], [e_d[p]])
            k.act(d2[p][:].rearrange("p h n -> p (h n)"), bpv[:, :, 63], AF.Exp, [e_bp], [e_d[p]])
            k.act(d3[p][:].rearrange("p h n -> p (h n)"), bbv[:, :, 31], AF.Exp, [e_bb], [e_d[p]])
            k.act(eb[:], bp[:], AF.Exp, [e_bp], [e_eb])
            k.act(enb[:], bp[:], AF.Exp, [e_bp], [e_enb], scale=-1.0)
            k.tt("dve", qt[p][:], qs[:], eb[:], ALU.mult, [e_qs, e_eb], [e_qt[p]])
            k.tt("pool", kt[:], kk[:], enb[:], ALU.mult, [e_kk, e_enb], [e_kt])

        def P2(tt):
            p = tt % 2
            bS = 0
            for n in range(8):
                csl = slice(n * 64, (n + 1) * 64)
                bK, bU = (1, 2) if n % 2 == 0 else (3, 4)
                po = 64 * (n % 2); blk = n // 2
                for hh in range(4):
                    k.mm(ps[bS][po:po + 64, hh * 64 + 32:(hh + 1) * 64], kt[:, hh, csl], qt[p][:, hh, n * 64 + 32:(n + 1) * 64], True, True,
                         [e_kt, e_qt[p]], [pe[bS]])
                    k.mm(ps[bS][po:po + 32, hh * 64:hh * 64 + 32], kt[:, hh, n * 64:n * 64 + 32], qt[p][:, hh, n * 64:n * 64 + 32], True, True,
                         [e_kt, e_qt[p]], [pe[bS]])
                for hh in range(4):
                    k.mm(ps[bK][po:po + 64, hh * 128:(hh + 1) * 128], kt[:, hh, csl], cb[:, CB_ID:CB_ID + 128], True, True, [e_kt, e_cb], [pe[bK]])
                k.tt("dve", scA[p][po:po + 64, n, :, :], ps[bS][po:po + 64, 0:256].rearrange("p (h c) -> p h c", c=64), cmask_bc(po), ALU.mult,
                     [e_cb], [pe[bS], e_scA[p][n]])
                k.act(ktok[n % 2][po:po + 64], ps[bK][po:po + 64, :].rearrange("p (h c) -> p h c", c=128), AF.Identity, [], [pe[bK], e_ktok[n % 2]])
                for hh in range(4):
                    k.mm(ps[bU][:, hh * 128:(hh + 1) * 128], ktok[n % 2][po:po + 64, hh, :], V64[p][po:po + 64, blk, hh * 128:(hh + 1) * 128], True, True,
                         [e_ktok[n % 2], e_V64[p]], [pe[bU]])
                k.tt("dve", tm2A[p][:, n, :, :], ps[bU][:].rearrange("p (h c) -> p h c", c=128), bc128(d2[p], n), ALU.mult, [e_d[p]],
                     [pe[bU], e_tm2A[p][n]])

        def S3(tt):
            p = tt % 2
            for n in range(8):
                k.tt("dve", stbfA[:, n, :, :], st32[:], bc128(d3[p], n), ALU.mult, [e_st, e_d[p]], [e_stbfA[n]])
                for hh in range(4):
                    k.stt(st32[:, hh, :], st32[:, hh, :], d1[p][:, hh, n:n + 1], tm2A[p][:, n, hh, :], ALU.mult, ALU.add,
                          [e_d[p], e_tm2A[p][n]], [e_st])

        def Q4(tt):
            p = tt % 2
            for n2 in range(4):
                bO = 5 + n2 % 2
                for n in (2 * n2, 2 * n2 + 1):
                    csl = slice(n * 64, (n + 1) * 64)
                    for hh in range(4):
                        oc0 = (n % 2) * 256 + hh * 64
                        po = 64 * (n % 2); blk = n // 2
                        k.mm(ps[bO][:, oc0:oc0 + 64], V64[p][po:po + 64, blk, hh * 128:(hh + 1) * 128], scA[p][po:po + 64, n, hh, :], True, False,
                             [e_V64[p], e_scA[p][n]], [pe[bO]])
                        k.mm(ps[bO][:, oc0:oc0 + 64], stbfA[:, n, hh, :], qt[p][:, hh, csl], False, True, [e_stbfA[n], e_qt[p]], [pe[bO]])
                k.act(oT[:, :, n2 * 128:(n2 + 1) * 128].rearrange("p h (n c) -> p n h c", c=64),
                      ps[bO][:].rearrange("p (n h c) -> p n h c", n=2, h=4), AF.Identity, [], [pe[bO], e_oT])
            osq = qt[p]; e_osq = e_qt[p]
            k.act(osq[:], oT[:], AF.Square, [e_oT], [e_osq])
            for hh in range(4):
                b = bank((4, 5, 6, 7))
                k.mm(ps[b][:], cb[:, CB_O128:CB_O128 + 128], osq[:, hh, :], True, True, [e_cb, e_osq], [pe[b]])
                k.act(ors[:], ps[b][:], AF.Ln, [e_cf], [pe[b], e_ors], bias=EPS)
                k.act(ors[:], ors[:], AF.Exp, [], [e_ors], scale=-0.5)
                k.stt(ot2[:, hh, :], oT[:, hh, :], vec[:, l, 232:233], ors[:], ALU.mult, ALU.mult, [e_oT, e_ors, e_vec], [e_ot2])
                k.tt("pool", oh[:, hh, :], ot2[:, hh, :], gs[p][:, hh, :], ALU.mult, [e_ot2, e_gs[p]], [e_oh])
            k.dma("sp", OH[tt], oh[:], [e_oh], [e_OH])

        k.dma("sp", htl[0][:], HT[0], [e_HT], [e_htl[0]])
        if NT > 1:
            k.dma("sp", htl[1][:], HT[1], [e_HT], [e_htl[1]])
        P1(0)
        P2(0)
        for tt in range(NT):
            S3(tt)
            if tt + 1 < NT:
                P1(tt + 1)
            if tt + 2 < NT:
                k.dma("sp", htl[tt % 2][:], HT[tt + 2], [e_HT], [e_htl[tt % 2]])
            Q4(tt)
            if tt + 1 < NT:
                P2(tt + 1)
        P.pop(); P.barrier()
        if f"OH{l}" in dbg_d:
            dbg_dump(f"OH{l}", OH, e_OH, 4, True)
        P.push()
        wpr = [P.sb([128, 4, D], BF16) for _ in range(3)]; e_wpr = [Ent() for _ in range(3)]
        wo = P.sb([128, 8, D], BF16); e_wo = Ent()
        for q3, wsrc in enumerate((w_cp, w_hp, w_sp)):
            k.dma("pool", wpr[q3][:], wview(wsrc[l], 0, D), [], [e_wpr[q3]])
        k.dma("pool", wo[:], wview(w_out[l], 0, D), [], [e_wo])
        obr = [[P.sb([128, 4, TT], BF16) for _ in range(3)] for _ in range(2)]
        e_obr = [[Ent() for _ in range(3)] for _ in range(2)]
        gpl = [P.sb([128, 3, TT], F32) for _ in range(4)]; e_gpl = [Ent() for _ in range(4)]
        gsb = [P.sb([128, TT], F32) for _ in range(3)]; e_gsb = [Ent() for _ in range(3)]
        xt = [P.sb([128, 8, TT], F32) for _ in range(2)]; e_xt = [Ent(), Ent()]
        mg = [P.sb([128, 8, TT], BF16) for _ in range(2)]; e_mgc = [[Ent() for _ in range(8)] for _ in range(2)]
        ma = [[P.sb([128, TT], F32) for _ in range(3)] for _ in range(2)]; e_ma = [[Ent() for _ in range(3)] for _ in range(2)]
        def loadE(tt_):
            b_ = tt_ % 2
            for i3, (src, esrc) in enumerate(((OC, e_OC), (OH, e_OH), (OS, e_OS))):
                k.dma("sp", obr[b_][i3][:], src[tt_], [esrc], [e_obr[b_][i3]])
            k.dma("sp", xt[b_][:], x_src[tt_], [e_xsrc], [e_xt[b_]])

        gq = [(tt_, c_) for tt_ in range(NT) for c_ in range(8)]

        def loadG(qi):
            if qi < len(gq):
                tt_, c_ = gq[qi]
                k.dma("sp", gpl[qi % 4][:], GP[tt_].rearrange("p (i c) t -> p i c t", c=8)[:, :, c_, :], [e_GP[tt_]], [e_gpl[qi % 4]])

        def Yst(tt):
            b2 = tt % 2
            for c in range(8):
                bs = [bank(), bank(), bank()]
                loadG(tt * 8 + c + 3)
                for i3 in range(3):
                    ch = i3 * 8 + c
                    k.act(gsb[i3][:], gpl[(tt * 8 + c) % 4][:, i3, :], AF.Sigmoid, [e_vec, e_gpl[(tt * 8 + c) % 4]], [e_gsb[i3]], bias=vec[:, l, 64 + ch:65 + ch])
                    for kc in range(4):
                        k.mm(ps[bs[i3]][:], wpr[i3][:, kc, c * 128:(c + 1) * 128], obr[b2][i3][:, kc, :], kc == 0, kc == 3,
                             [e_wpr[i3], e_obr[b2][i3]], [pe[bs[i3]]])
                mq = ma[c % 2]; e_mq = e_ma[c % 2]
                for i3 in range(3):
                    k.tt("dve", mq[i3][:], ps[bs[i3]][:], gsb[i3][:], ALU.mult, [e_gsb[i3]], [pe[bs[i3]], e_mq[i3]])
                k.tt("dve", mq[0][:], mq[0][:], mq[1][:], ALU.add, [e_mq[1]], [e_mq[0]])
                k.tt("pool", mg[b2][:, c, :], mq[0][:], mq[2][:], ALU.add, [e_mq[0], e_mq[2]], [e_mgc[b2][c]])

        def Wst(tt):
            b2 = tt % 2
            for c in range(8):
                b = bank()
                for kc in range(8):
                    k.mm(ps[b][:], wo[:, kc, c * 128:(c + 1) * 128], mg[b2][:, kc, :], kc == 0, kc == 7, [e_wo, e_mgc[b2][kc]], [pe[b]])
                k.stt(xt[b2][:, c, :], ps[b][:], dv[:, l, 16 + c:17 + c], xt[b2][:, c, :], ALU.mult, ALU.add, [e_dv], [pe[b], e_xt[b2]])
            k.dma("sp", xm_s[tt], xt[b2][:], [e_xt[b2]], [e_xm])

        loadE(0)
        if NT > 1:
            loadE(1)
        loadG(0); loadG(1); loadG(2)
        Yst(0)
        for tt in range(NT):
            if tt + 1 < NT:
                Yst(tt + 1)
            Wst(tt)
            if tt + 2 < NT:
                loadE(tt + 2)
        P.pop(); P.barrier()
        if f"xm{l}" in dbg_d:
            dbg_dump(f"xm{l}", xm_s, e_xm, 8, False)

        P.push()
        w1 = P.sb([128, 8, 4 * D], BF16); e_w1q = [Ent() for _ in range(4)]
        w2 = P.sb([128, 32, D], BF16); e_w2q = [Ent() for _ in range(4)]
        for q4 in range(4):
            k.dma("pool", w1[:, :, q4 * 1024:(q4 + 1) * 1024], wview(w1_d[l], q4 * 1024, (q4 + 1) * 1024), [], [e_w1q[q4]])
        for q4 in range(4):
            k.dma("pool", w2[:, q4 * 8:(q4 + 1) * 8, :], w2_d[l].rearrange("(kc p) n -> p kc n", p=128)[:, q4 * 8:(q4 + 1) * 8, :], [], [e_w2q[q4]])
        xt = P.sb([128, 8, TT], F32); e_xt = Ent()
        sqr = [P.sb([128, TT], BF16) for _ in range(2)]; e_sqr = [Ent(), Ent()]
        rs = P.sb([128, TT], F32); e_rs = Ent()
        tmp = [P.sb([128, TT], F32) for _ in range(2)]; e_tmp = [Ent(), Ent()]
        h2 = P.sb([128, 8, TT], BF16); e_h2 = Ent()
        hid = P.sb([128, 32, TT], BF16); e_hidj = [Ent() for _ in range(32)]
        rl = P.sb([128, TT], F32); e_rl = Ent()
        xr = [P.sb([128, TT], F32) for _ in range(2)]; e_xr = [Ent(), Ent()]

        def normF():
            norm_mod(xt, e_xt, lambda kc: dv[:, l, 56 + kc:57 + kc], lambda kc: dv[:, l, 24 + kc:25 + kc],
                     lambda kc: h2[:, kc, :], e_h2, sqr, e_sqr, rs, e_rs, tmp, e_tmp)

        k.dma("sp", xt[:], xm_s[0], [e_xm], [e_xt])
        normF()
        if NT > 1:
            k.dma("sp", xt[:], xm_s[1], [e_xm], [e_xt])
        for tt in range(NT):
            for j in range(32):
                b = bank()
                for kc in range(8):
                    k.mm(ps[b][:], w1[:, kc, j * 128:(j + 1) * 128], h2[:, kc, :], kc == 0, kc == 7, [e_w1q[j // 8], e_h2], [pe[b]])
                k.act(rl[:], ps[b][:], AF.Relu, [], [pe[b], e_rl])
                k.tt("pool" if j % 2 else "dve", hid[:, j, :], rl[:], rl[:], ALU.mult, [e_rl], [e_hidj[j]])
            k.dma("sp", xr[0][:], xm_s[tt][:, 0, :], [e_xm], [e_xr[0]])
            if tt + 1 < NT:
                normF()
                if tt + 2 < NT:
                    k.dma("sp", xt[:], xm_s[tt + 2], [e_xm], [e_xt])
            for c in range(8):
                if c + 1 < 8:
                    k.dma("sp", xr[(c + 1) % 2][:], xm_s[tt][:, c + 1, :], [e_xm], [e_xr[(c + 1) % 2]])
                b = bank()
                for j in range(32):
                    k.mm(ps[b][:], w2[:, j, c * 128:(c + 1) * 128], hid[:, j, :], j == 0, j == 31, [e_w2q[j // 8], e_hidj[j]], [pe[b]])
                k.stt(xr[c % 2][:], ps[b][:], dv[:, l, 40 + c:41 + c], xr[c % 2][:], ALU.mult, ALU.add, [e_dv], [pe[b], e_xr[c % 2]])
                k.dma("sp", x_dst[tt][:, c, :], xr[c % 2][:], [e_xr[c % 2]], [e_xs])
        P.pop(); P.barrier()
    P.barrier()
    P.emit()
    global LASTP
    LASTP = P
    return nc


def make_inputs(inp, B0, S, L, shared=None):
    if shared is None:
        cbv, cfv = host_consts()
        shared = {
            "cbf": cbv, "cf32": cfv,
            "vecs": np.ascontiguousarray(np.stack([host_vecs(inp, l) for l in range(L)], axis=1)),
        }
        for n in ("mod_w", "w_in", "w_conv_proj", "w_hgrn_proj", "w_sb_proj", "w_out", "mlp_w1", "mlp_w2"):
            shared[n] = np.ascontiguousarray(np.asarray(inp[n], np.float32)[:L])
    m = dict(shared)
    m["xT"] = np.ascontiguousarray(np.asarray(inp["x"][B0], np.float32)[:S].reshape(S // TT, TT, 8, 128).transpose(0, 3, 2, 1))
    m["cc"] = np.ascontiguousarray(np.asarray(inp["c"][B0], np.float32).reshape(8, 128).T)
    return m


_CACHE = {}


def untile(o):
    nt = o.shape[0]
    return np.ascontiguousarray(o.transpose(0, 3, 2, 1).reshape(nt * TT, 1024))


def kernel(**inputs):
    B, S, _ = inputs["x"].shape
    L = int(np.asarray(inputs["mod_w"]).shape[0])
    key = (S, L)
    if key not in _CACHE:
        _CACHE[key] = build(S, L)
    nc = _CACHE[key]
    inp = {k_: np.asarray(v) for k_, v in inputs.items()}
    first = make_inputs(inp, 0, S, L)
    shared = {k_: v for k_, v in first.items() if k_ not in ("xT", "cc")}
    in_maps = [first] + [make_inputs(inp, b, S, L, shared=shared) for b in range(1, B)]
    res = run_bass_kernel_spmd(nc, in_maps, core_ids=list(range(B)))
    out = np.stack([untile(np.asarray(r["outT"], np.float32)) for r in res.results], axis=0)
    return out.astype(np.float32)
```

```python
import numpy as np
import ml_dtypes
import concourse.bass as bass
import concourse.mybir as mybir
from concourse.bass_utils import run_bass_kernel_spmd

F32 = mybir.dt.float32
BF16 = mybir.dt.bfloat16
AF = mybir.ActivationFunctionType
ALU = mybir.AluOpType
NPBF = ml_dtypes.bfloat16

ENGS = ("pe", "act", "dve", "pool", "sp")
EPOCH = 24000


class Ent:
    __slots__ = ("w", "r")

    def __init__(self):
        self.w = None
        self.r = {}


class Op:
    __slots__ = ("fn", "deps", "dma", "sig", "slot", "seq")

    def __init__(self, fn, deps, dma, slot=0, seq=0):
        self.fn = fn
        self.deps = deps
        self.dma = dma
        self.sig = dma
        self.slot = slot
        self.seq = seq


NSLOT = 8


class Prog:
    def __init__(self, nc):
        self.nc = nc
        self.ops = {e: [] for e in ENGS}
        self.dmas = {e: [] for e in ENGS}
        self.sb_off = 16640
        self.sb_mark = []
        self.n_t = 0

    def sb(self, shape, dtype, name=None):
        esz = 2 if dtype == BF16 else 4
        n = 1
        for s in shape[1:]:
            n *= s
        nbytes = (n * esz + 63) // 64 * 64
        off = self.sb_off
        self.sb_off += nbytes
        assert self.sb_off <= 229376, f"SBUF overflow {self.sb_off}"
        self.n_t += 1
        return self.nc.alloc_sbuf_tensor_at(f"{name or 't'}{self.n_t}", list(shape), dtype, offset=off)

    def push(self):
        self.sb_mark.append(self.sb_off)

    def pop(self):
        self.sb_off = self.sb_mark.pop()

    def add(self, eng, fn, reads=(), writes=(), dma=False):
        idx = len(self.ops[eng])
        tok = (eng, idx)
        deps = set()
        for e in reads:
            if e.w is not None:
                deps.add(e.w)
        for e in writes:
            if e.w is not None:
                deps.add(e.w)
            for t in e.r.values():
                deps.add(t)
        deps.discard(tok)
        slot = seq = 0
        if dma:
            seq = len(self.dmas[eng])
            slot = seq % NSLOT
            if seq >= NSLOT:
                deps.add((eng, self.dmas[eng][seq - NSLOT]))
            self.dmas[eng].append(idx)
        rk = (eng, "d", slot) if dma else (eng, "c")
        for e in reads:
            e.r[rk] = tok
        for e in writes:
            e.w = tok
            e.r = {}
        self.ops[eng].append(Op(fn, deps, dma, slot, seq))
        return tok

    def barrier(self):
        last = []
        for e in ENGS:
            seen_c = False
            seen_d = set()
            for i in range(len(self.ops[e]) - 1, -1, -1):
                op = self.ops[e][i]
                if op.fn is None:
                    continue
                if op.dma:
                    if op.slot not in seen_d:
                        last.append((e, i))
                        seen_d.add(op.slot)
                elif not seen_c:
                    last.append((e, i))
                    seen_c = True
                if seen_c and len(seen_d) == NSLOT:
                    break
        for e in ENGS:
            self.ops[e].append(Op(None, set(last), False))
            self.ops[e][-1].sig = False

    def emit(self):
        nc = self.nc
        ops = self.ops
        for e in ENGS:
            for i, op in enumerate(ops[e]):
                for (e2, j) in op.deps:
                    if e2 == "pe" and e == "pe":
                        continue
                    ops[e2][j].sig = True
        tokval = {}
        sems = {}
        for e in ENGS:
            cnt = 0
            for i, op in enumerate(ops[e]):
                if op.fn is None or not op.sig:
                    continue
                if op.dma:
                    n = op.seq // NSLOT
                    per = EPOCH // 16
                    k = (e, "d", op.slot, n // per)
                    v = (n % per + 1) * 16
                else:
                    k = (e, "c", 0, cnt // EPOCH)
                    v = cnt % EPOCH + 1
                    cnt += 1
                tokval[(e, i)] = (k, v)
                if k not in sems:
                    sems[k] = nc.alloc_semaphore("s_" + "_".join(str(x) for x in k))
        self.n_sems = len(sems)

        def run(e, eng):
            waited = {}
            for i, op in enumerate(ops[e]):
                need = {}
                for (e2, j) in op.deps:
                    if e2 == "pe" and e == "pe":
                        continue
                    k, v = tokval[(e2, j)]
                    if need.get(k, 0) < v:
                        need[k] = v
                for k, v in need.items():
                    if waited.get(k, 0) >= v:
                        continue
                    if any(kk[:3] == k[:3] and kk[3] > k[3] for kk in waited):
                        continue
                    eng.wait_ge(sems[k], v)
                    waited[k] = v
                if op.fn is None:
                    continue
                ins = op.fn(eng)
                if op.sig:
                    k, v = tokval[(e, i)]
                    ins.then_inc(sems[k], 16 if op.dma else 1)

        with nc.Block() as block:
            @block.tensor
            def _(eng):
                run("pe", eng)

            @block.scalar
            def _(eng):
                run("act", eng)

            @block.vector
            def _(eng):
                run("dve", eng)

            @block.gpsimd
            def _(eng):
                run("pool", eng)

            @block.sync
            def _(eng):
                run("sp", eng)


D = 1024
KC = 8
TT = 512
C_CA, C_CG, C_HQ, C_HF, C_HI, C_HG, C_SQ, C_SK, C_SV, C_GL = 0, 512, 1024, 1536, 2048, 2560, 3072, 3584, 4096, 4608
NV = 240
CB_ID, CB_OD, CB_O512, CB_BD64, CB_O128, CB_NTRI, CB_NONE, CB_CMASK, CB_NMASK, NCB = 0, 128, 256, 384, 512, 640, 768, 896, 960, 3008
NCF = 516
USE_LO = False


def host_consts():
    cb = np.zeros((128, NCB), np.float32)
    p = np.arange(128)
    cb[:, CB_ID:CB_ID + 128] = np.eye(128)
    cb[:, CB_OD:CB_OD + 128] = 1.0 / 1024
    cb[:, CB_O512:CB_O512 + 128] = 1.0 / 512
    cb[:, CB_BD64:CB_BD64 + 128] = (p[:, None] // 64 == p[None, :] // 64) / 64.0
    cb[:, CB_O128:CB_O128 + 128] = 1.0 / 128
    cb[:, CB_NTRI:CB_NTRI + 128] = -(p[:, None] >= p[None, :]).astype(np.float32)
    cb[:, CB_NONE:CB_NONE + 128] = -1.0
    t = np.arange(512)
    for r in range(4):
        valid = (128 * r + p[:, None]) < t[None, :]
        cb[:, CB_NMASK + r * 512:CB_NMASK + (r + 1) * 512] = np.where(valid, 0.0, -30000.0)
    cb[:64, CB_CMASK:CB_CMASK + 64] = (p[:64, None] <= p[None, :64]).astype(np.float32)
    cb[64:, CB_CMASK:CB_CMASK + 64] = cb[:64, CB_CMASK:CB_CMASK + 64]
    cf = np.ones((128, NCF), np.float32)
    cf[:, 0:512:64] = 0.0
    cf[:, 512] = 1e-6
    cf[:, 513] = 1.0
    cf[:, 514] = 0.0
    return cb.astype(NPBF), cf


def host_vecs(inp, l):
    v = np.zeros((128, NV), np.float32)
    col = lambda a: np.asarray(a, np.float32).reshape(-1, 128).T
    v[:, 0:48] = col(inp["mod_b"][l])
    v[:, 48:56] = col(inp["norm1_g"][l])
    v[:, 56:64] = col(inp["norm2_g"][l])
    v[:, 64:88] = col(inp["gate_b"][l])
    v[:, 88:92] = col(inp["conv_b"][l])
    v[:, 92:96] = col(inp["conv_ln_g"][l])
    v[:, 96:100] = col(inp["conv_ln_b"][l])
    cw = np.asarray(inp["conv_w"][l], np.float32)
    v[:, 100:224] = cw.T.reshape(4, 128, 31).transpose(1, 0, 2).reshape(128, 124)
    v[:, 224:228] = col(inp["hgrn_lb"][0])
    v[:, 228:232] = col(inp["hgrn_lb"][1])
    v[:, 232] = np.asarray(inp["hgrn_norm_g"][l], np.float32)
    v[:, 233] = np.tile(np.asarray(inp["sb_qn_g"][l], np.float32), 2)
    v[:, 234] = np.tile(np.asarray(inp["sb_kn_g"][l], np.float32), 2)
    return v


class K:
    def __init__(self, P):
        self.P = P

    def mm(self, out, lhsT, rhs, start, stop, reads, writes, skip=False):
        self.P.add("pe", lambda g: g.matmul(out, lhsT=lhsT, rhs=rhs, start=start, stop=stop, skip_group_check=skip),
                   reads, writes)

    def act(self, out, in_, func, reads, writes, scale=None, bias=None):
        kw = {}
        if scale is not None:
            kw["scale"] = scale
        if bias is not None:
            kw["bias"] = bias
        self.P.add("act", lambda g: g.activation(out=out, in_=in_, func=func, **kw), reads, writes)

    def tt(self, eng, out, in0, in1, op, reads, writes):
        self.P.add(eng, lambda g: g.tensor_tensor(out=out, in0=in0, in1=in1, op=op), reads, writes)

    def ts(self, eng, out, in0, s1, op0, reads, writes, s2=None, op1=None):
        if op1 is None:
            self.P.add(eng, lambda g: g.tensor_scalar(out=out, in0=in0, scalar1=s1, scalar2=None, op0=op0), reads, writes)
        else:
            self.P.add(eng, lambda g: g.tensor_scalar(out=out, in0=in0, scalar1=s1, scalar2=s2, op0=op0, op1=op1),
                       reads, writes)

    def stt(self, out, in0, scalar, in1, op0, op1, reads, writes):
        self.P.add("dve", lambda g: g.scalar_tensor_tensor(out=out, in0=in0, scalar=scalar, in1=in1, op0=op0, op1=op1),
                   reads, writes)

    def copy(self, eng, out, in_, reads, writes):
        self.P.add(eng, lambda g: g.tensor_copy(out=out, in_=in_), reads, writes)

    def memset(self, eng, ap, val, writes):
        self.P.add(eng, lambda g: g.memset(ap, val), (), writes)

    def recip(self, out, in_, reads, writes):
        self.P.add("dve", lambda g: g.reciprocal(out=out, in_=in_), reads, writes)

    def dma(self, q, out, in_, reads, writes):
        self.P.add(q, lambda g: g.dma_start(out=out, in_=in_), reads, writes, dma=True)


def bcast_last(ap3, n):
    a = ap3.ap
    return bass.AP(ap3.tensor, ap3.offset, [list(a[0]), list(a[1]), [0, n]])


def build(S, L=2, dbg=()):
    NT = S // TT
    nc = bass.Bass("TRN2", target_bir_lowering=False)
    P = Prog(nc)
    k = K(P)

    def dram(name, shape, dt, kind="Internal"):
        return nc.dram_tensor(name, list(shape), dt, kind=kind).ap()

    xT_in = dram("xT", [NT, 128, 8, TT], F32, "ExternalInput")
    outT = dram("outT", [NT, 128, 8, TT], F32, "ExternalOutput")
    cc_d = dram("cc", [128, 8], F32, "ExternalInput")
    cb_d = dram("cbf", [128, NCB], BF16, "ExternalInput")
    cf_d = dram("cf32", [128, NCF], F32, "ExternalInput")
    vec_d = dram("vecs", [128, L, NV], F32, "ExternalInput")
    mod_w = dram("mod_w", [L, D, 6 * D], F32, "ExternalInput")
    w_in = dram("w_in", [L, D, 7680], F32, "ExternalInput")
    w_cp = dram("w_conv_proj", [L, 512, D], F32, "ExternalInput")
    w_hp = dram("w_hgrn_proj", [L, 512, D], F32, "ExternalInput")
    w_sp = dram("w_sb_proj", [L, 512, D], F32, "ExternalInput")
    w_out = dram("w_out", [L, D, D], F32, "ExternalInput")
    w1_d = dram("mlp_w1", [L, D, 4 * D], F32, "ExternalInput")
    w2_d = dram("mlp_w2", [L, 4 * D, D], F32, "ExternalInput")
    xm_s = dram("xm_s", [NT, 128, 8, TT], F32)
    x_s = dram("x_s", [NT, 128, 8, TT], F32)
    HT = dram("HT", [NT, 128, 8, TT], BF16)
    OC = dram("OC", [NT, 128, 4, TT], BF16)
    OH = dram("OH", [NT, 128, 4, TT], BF16)
    OS = dram("OS", [NT, 128, 4, TT], BF16)
    GP = dram("GP", [NT, 128, 24, TT], F32)
    e_GP = [Ent() for _ in range(NT)]
    e_xm, e_xs, e_OC, e_OH, e_OS, e_HT = Ent(), Ent(), Ent(), Ent(), Ent(), Ent()
    dbg_d = {n: dram("dbg_" + n, shp, F32, "ExternalOutput") for n, shp in dbg}

    psall = nc.alloc_psum_tensor("psall", [128, 8, 512], F32)
    ps = [psall[:, i, :] for i in range(8)]
    pe = [Ent() for _ in range(8)]
    rot = {"i": 0}

    def bank(lst=(0, 1, 2, 3, 4, 5, 6, 7)):
        rot["i"] += 1
        return lst[rot["i"] % len(lst)]

    def wview(w2d, c0, c1):
        return w2d.rearrange("(kc p) n -> p kc n", p=128)[:, :, c0:c1]

    def fview(t4d, t0, t1):
        assert t0 % TT == 0 and t1 == t0 + TT
        return t4d[t0 // TT]

    def fview_rm(t2d, t0, t1):
        return t2d.rearrange("(kc p) t -> p kc t", p=128)[:, :, t0:t1]

    def dbg_dump(name, src4d, e_src, kcn, bf):
        P.push()
        for tt_ in range(NT):
            d32 = P.sb([128, kcn, TT], F32); e_d32 = Ent()
            if bf:
                d16 = P.sb([128, kcn, TT], BF16); e_d16 = Ent()
                k.dma("sp", d16[:], src4d[tt_], [e_src], [e_d16])
                k.copy("dve", d32[:], d16[:], [e_d16], [e_d32])
            else:
                k.dma("sp", d32[:], src4d[tt_], [e_src], [e_d32])
            k.dma("sp", fview_rm(dbg_d[name], tt_ * TT, (tt_ + 1) * TT), d32[:], [e_d32], [])
        P.pop(); P.barrier()


    cb = P.sb([128, CB_NMASK], BF16); e_cb = Ent()
    cf = P.sb([128, NCF], F32); e_cf = Ent()
    vec = P.sb([128, L, NV], F32); e_vec = Ent()
    dv = P.sb([128, L, 80], F32); e_dv = Ent()
    cc = P.sb([128, 8], F32); e_cc = Ent()
    cact = P.sb([128, 8], BF16); e_cact = Ent()
    k.dma("sp", cb[:], cb_d[:, 0:CB_NMASK], [], [e_cb])
    k.dma("sp", cf[:], cf_d, [], [e_cf])
    k.dma("sp", vec[:], vec_d, [], [e_vec])
    k.dma("sp", cc[:], cc_d, [], [e_cc])
    k.act(cact[:], cc[:], AF.Silu, [e_cc], [e_cact])
    EPS = cf[:, 512:513]
    ONE = cf[:, 513:514]
    def mod_dma(l_, grp, mw, e_mw):
        b = grp % 2
        k.dma("pool", mw[b][:], wview(mod_w[l_], grp * 1024, (grp + 1) * 1024), [], [e_mw[b]])

    def mod_group(l_, grp, mw, e_mw, dma=True):
        b = grp % 2
        if dma:
            mod_dma(l_, grp, mw, e_mw)
        for j in range(8):
            col = grp * 8 + j
            for kc in range(8):
                k.mm(ps[7][:, col:col + 1], mw[b][:, kc, j * 128:(j + 1) * 128], cact[:, kc:kc + 1], kc == 0, kc == 7,
                     [e_mw[b], e_cact], [pe[7]])

    def mod_finish(l_):
        k.tt("dve", dv[:, l_, 0:48], ps[7][:, 0:48], vec[:, l_, 0:48], ALU.add, [e_vec], [pe[7], e_dv])
        k.stt(dv[:, l_, 48:56], dv[:, l_, 8:16], 1.0, vec[:, l_, 48:56], ALU.add, ALU.mult, [e_vec], [e_dv])
        k.stt(dv[:, l_, 56:64], dv[:, l_, 32:40], 1.0, vec[:, l_, 56:64], ALU.add, ALU.mult, [e_vec], [e_dv])
        k.ts("dve", dv[:, l_, 72:73], vec[:, l_, 233:234], 0.125, ALU.mult, [e_vec], [e_dv])

    P.push()
    mw0 = [P.sb([128, 8, 1024], BF16) for _ in range(2)]
    e_mw0 = [Ent(), Ent()]
    for grp in range(6):
        mod_group(0, grp, mw0, e_mw0)
    mod_finish(0)
    k.memset("dve", dv[:, 0, 64:68], 1.0, [e_dv])
    k.memset("dve", dv[:, 0, 68:72], -1.0, [e_dv])
    k.memset("dve", dv[:, 0, 76:80], 0.0, [e_dv])
    if L > 1:
        k.tt("dve", dv[:, 1, 76:80], vec[:, 0, 224:228], vec[:, 0, 228:232], ALU.subtract, [e_vec], [e_dv])
        k.act(dv[:, 1, 64:68], dv[:, 1, 76:80], AF.Sigmoid, [], [e_dv])
        k.ts("dve", dv[:, 1, 68:72], dv[:, 1, 64:68], -1.0, ALU.mult, [], [e_dv])
        k.ts("dve", dv[:, 1, 76:80], dv[:, 1, 64:68], -1.0, ALU.mult, [], [e_dv], s2=1.0, op1=ALU.add)
    P.pop()
    P.barrier()
    if "dv" in dbg_d:
        k.dma("sp", dbg_d["dv"], dv[:], [e_dv], [])

    def norm_mod(xt, e_xt, acol, shcol, dst_fn, e_dst, sq, e_sq, rs, e_rs, tmp, e_tmp):
        b = bank()
        if isinstance(sq, list):
            for kc in range(8):
                k.act(sq[kc % 2][:], xt[:, kc, :], AF.Square, [e_xt], [e_sq[kc % 2]])
                k.mm(ps[b][:], cb[:, CB_OD:CB_OD + 128], sq[kc % 2][:], kc == 0, kc == 7, [e_cb, e_sq[kc % 2]], [pe[b]])
        else:
            k.act(sq[:], xt[:], AF.Square, [e_xt], [e_sq])
            for kc in range(8):
                k.mm(ps[b][:], cb[:, CB_OD:CB_OD + 128], sq[:, kc, :], kc == 0, kc == 7, [e_cb, e_sq], [pe[b]])
        k.act(rs[:], ps[b][:], AF.Ln, [e_cf], [pe[b], e_rs], bias=EPS)
        k.act(rs[:], rs[:], AF.Exp, [], [e_rs], scale=-0.5)
        for kc in range(8):
            i2 = kc % 2
            k.stt(tmp[i2][:], xt[:, kc, :], acol(kc), rs[:], ALU.mult, ALU.mult, [e_xt, e_rs, e_dv], [e_tmp[i2]])
            if kc % 2 == 0:
                k.act(dst_fn(kc), tmp[i2][:], AF.Identity, [e_tmp[i2], e_dv], [e_dst], bias=shcol(kc))
            else:
                k.ts("dve", dst_fn(kc), tmp[i2][:], shcol(kc), ALU.add, [e_tmp[i2], e_dv], [e_dst])

    for l in range(L):
        x_src, e_xsrc = (xT_in, Ent()) if l == 0 else (x_s, e_xs)
        x_dst = outT if l == L - 1 else x_s
        P.push()
        hTt = [P.sb([128, 8, TT], BF16) for _ in range(2)]; e_hTt = [Ent(), Ent()]
        xt = [P.sb([128, 8, TT], F32) for _ in range(2)]; e_xt = [Ent(), Ent()]
        sq = P.sb([128, 8, TT], BF16); e_sq = Ent()
        rs = P.sb([128, TT], F32); e_rs = Ent()
        tmp = [P.sb([128, TT], F32) for _ in range(2)]; e_tmp = [Ent(), Ent()]
        k.dma("sp", xt[0][:], fview(x_src, 0, TT), [e_xsrc], [e_xt[0]])
        for tt in range(NT):
            b = tt % 2
            if tt + 1 < NT:
                k.dma("sp", xt[1 - b][:], fview(x_src, (tt + 1) * TT, (tt + 2) * TT), [e_xsrc], [e_xt[1 - b]])
            norm_mod(xt[b], e_xt[b], lambda kc: dv[:, l, 48 + kc:49 + kc], lambda kc: dv[:, l, kc:kc + 1],
                     lambda kc: hTt[b][:, kc, :], e_hTt[b], sq, e_sq, rs, e_rs, tmp, e_tmp)
            k.dma("sp", HT[tt], hTt[b][:], [e_hTt[b]], [e_HT])
        P.pop(); P.barrier()
        if f"hT{l}" in dbg_d:
            dbg_dump(f"hT{l}", HT, e_HT, 8, True)

        P.push()
        wa = P.sb([128, 8, 1024], BF16); e_wa = Ent()
        k.dma("pool", wa[:], wview(w_in[l], C_CA, C_CA + 1024), [], [e_wa])
        dg = P.sb([128, 4, 31, 128], BF16); e_dg2 = [Ent(), Ent()]
        uT = P.sb([128, 4, 30 + S], BF16); e_u = [Ent() for _ in range(NT)]
        e_upad = Ent()
        k.memset("pool", uT[:, :, 0:30], 0.0, [e_upad])
        dg_todo = [(c, j) for c in range(4) for j in range(31)]

        def dg_some(nops):
            for _ in range(nops):
                if dg_todo:
                    c_, j_ = dg_todo.pop(0)
                    k.ts("dve", dg[:, c_, j_, :], cb[:, CB_ID:CB_ID + 128],
                         vec[:, l, 100 + c_ * 31 + j_:101 + c_ * 31 + j_], ALU.mult, [e_cb, e_vec], [e_dg2[0]])
        sgt = [P.sb([128, TT], F32) for _ in range(2)]; e_sgt = [Ent(), Ent()]
        htl = [P.sb([128, 8, TT], BF16) for _ in range(2)]; e_htl = [Ent(), Ent()]
        k.dma("sp", htl[0][:], HT[0], [e_HT], [e_htl[0]])
        for tt in range(NT):
            if tt + 1 < NT:
                k.dma("sp", htl[(tt + 1) % 2][:], HT[tt + 1], [e_HT], [e_htl[(tt + 1) % 2]])
            for c in range(4):
                ba, bg = bank(), bank()
                for kc in range(8):
                    k.mm(ps[ba][:], wa[:, kc, c * 128:(c + 1) * 128], htl[tt % 2][:, kc, :], kc == 0, kc == 7,
                         [e_wa, e_htl[tt % 2]], [pe[ba]])
                for kc in range(8):
                    k.mm(ps[bg][:], wa[:, kc, 512 + c * 128:512 + (c + 1) * 128], htl[tt % 2][:, kc, :], kc == 0, kc == 7,
                         [e_wa, e_htl[tt % 2]], [pe[bg]])
                i2 = c % 2
                k.act(sgt[i2][:], ps[bg][:], AF.Sigmoid, [], [pe[bg], e_sgt[i2]])
                k.tt("dve", uT[:, c, 30 + tt * TT:30 + (tt + 1) * TT], ps[ba][:], sgt[i2][:], ALU.mult, [e_sgt[i2]], [pe[ba], e_u[tt]])
                dg_some((124 + 4 * NT - 1) // (4 * NT))
        dg_some(124)
        defer_mod = (l == 0 and L > 1)
        if defer_mod:
            mw1 = [P.sb([128, 8, 1024], BF16) for _ in range(2)]
            e_mw1 = [Ent(), Ent()]
            mod_dma(1, 0, mw1, e_mw1)
            mod_dma(1, 1, mw1, e_mw1)
        v32 = P.sb([128, 4, TT], F32); e_v32 = Ent()
        vbf = P.sb([128, 4, TT], BF16); e_vbf = Ent()
        vsq = P.sb([128, 4, TT], BF16); e_vsq = Ent()
        mu = P.sb([128, TT], F32); e_mu = Ent()
        var = P.sb([128, TT], F32); e_var = Ent()
        t1 = P.sb([128, 4, TT], F32); e_t1 = Ent()
        oc = [P.sb([128, 4, TT], BF16) for _ in range(2)]; e_oc = [Ent(), Ent()]
        for tt in range(NT):
            for c in range(4):
                b = bank((0, 1, 2, 3, 4, 5, 6))
                rd = [e_dg2[0], e_dg2[1], e_u[tt], e_upad] + ([e_u[tt - 1]] if tt > 0 else [])
                for j in range(31):
                    k.mm(ps[b][:], dg[:, c, j, :], uT[:, c, tt * TT + j:tt * TT + j + TT], j == 0, j == 30, rd, [pe[b]])
                k.act(v32[:, c, :], ps[b][:], AF.Identity, [e_vec], [pe[b], e_v32], bias=vec[:, l, 88 + c:89 + c])
            k.copy("pool", vbf[:], v32[:], [e_v32], [e_vbf])
            k.act(vsq[:], v32[:], AF.Square, [e_v32], [e_vsq])
            bm, bq = bank((0, 1, 2, 3, 4, 5, 6)), bank((0, 1, 2, 3, 4, 5, 6))
            for c in range(4):
                k.mm(ps[bm][:], cb[:, CB_O512:CB_O512 + 128], vbf[:, c, :], c == 0, c == 3, [e_cb, e_vbf], [pe[bm]])
            for c in range(4):
                k.mm(ps[bq][:], cb[:, CB_O512:CB_O512 + 128], vsq[:, c, :], c == 0, c == 3, [e_cb, e_vsq], [pe[bq]])
            k.act(mu[:], ps[bm][:], AF.Identity, [], [pe[bm], e_mu])
            k.stt(var[:], mu[:], -1.0, mu[:], ALU.mult, ALU.mult, [e_mu], [e_var])
            k.tt("dve", var[:], ps[bq][:], var[:], ALU.add, [], [pe[bq], e_var])
            k.ts("dve", var[:], var[:], 0.0, ALU.max, [], [e_var])
            k.act(var[:], var[:], AF.Ln, [e_cf], [e_var], bias=EPS)
            k.act(var[:], var[:], AF.Exp, [], [e_var], scale=-0.5)
            ob = tt % 2
            for c in range(4):
                k.tt("dve", t1[:, c, :], v32[:, c, :], mu[:], ALU.subtract, [e_v32, e_mu], [e_t1])
                k.tt("pool", t1[:, c, :], t1[:, c, :], var[:], ALU.mult, [e_var], [e_t1])
                k.act(oc[ob][:, c, :], t1[:, c, :], AF.Silu, [e_t1, e_vec], [e_oc[ob]],
                      scale=vec[:, l, 92 + c:93 + c], bias=vec[:, l, 96 + c:97 + c])
            k.dma("sp", fview(OC, tt * TT, (tt + 1) * TT), oc[ob][:], [e_oc[ob]], [e_OC])
            if defer_mod:
                grps = list(range(6 * tt // NT, 6 * (tt + 1) // NT))
                for gi, grp in enumerate(grps):
                    mod_group(1, grp, mw1, e_mw1, dma=False)
                    if grp + 2 < 6:
                        mod_dma(1, grp + 2, mw1, e_mw1)
        if defer_mod:
            mod_finish(1)
        P.pop(); P.barrier()
        if f"OC{l}" in dbg_d:
            dbg_dump(f"OC{l}", OC, e_OC, 4, True)
        P.push()
        QT = P.sb([128, 4, S], BF16); KT = P.sb([128, 4, S], BF16)
        VV = P.sb([128, S // 128, 512], BF16)
        e_Q = [[Ent() for _ in range(NT)] for _ in range(4)]
        e_K = [[Ent() for _ in range(NT)] for _ in range(4)]
        e_V = [Ent() for _ in range(S // 128)]
        P.push()
        wqk = P.sb([128, 8, 1024], BF16); e_wqk = Ent()
        wv = P.sb([128, 8, 512], BF16); e_wv = Ent()
        k.dma("pool", wqk[:], wview(w_in[l], C_SQ, C_SQ + 1024), [], [e_wqk])
        k.dma("pool", wv[:], wview(w_in[l], C_SV, C_SV + 512), [], [e_wv])
        sqb = [P.sb([128, TT], BF16) for _ in range(2)]; e_sqb = [Ent(), Ent()]
        q32 = [P.sb([128, TT], F32) for _ in range(2)]; e_q32 = [Ent(), Ent()]
        rsb = [P.sb([128, TT], F32) for _ in range(2)]; e_rsb = [Ent(), Ent()]
        it = 0
        htl = [P.sb([128, 8, TT], BF16) for _ in range(2)]; e_htl = [Ent(), Ent()]
        k.dma("sp", htl[0][:], HT[0], [e_HT], [e_htl[0]])
        for tt in range(NT):
            if tt + 1 < NT:
                k.dma("sp", htl[(tt + 1) % 2][:], HT[tt + 1], [e_HT], [e_htl[(tt + 1) % 2]])
            for which in range(2):
                for c in range(4):
                    i2 = it % 2; it += 1
                    b, b2 = bank(), bank()
                    for kc in range(8):
                        k.mm(ps[b][:], wqk[:, kc, which * 512 + c * 128:which * 512 + (c + 1) * 128],
                             htl[tt % 2][:, kc, :], kc == 0, kc == 7, [e_wqk, e_htl[tt % 2]], [pe[b]])
                    k.act(sqb[i2][:], ps[b][:], AF.Square, [], [pe[b], e_sqb[i2]])
                    k.copy("dve", q32[i2][:], ps[b][:], [], [pe[b], e_q32[i2]])
                    k.mm(ps[b2][:], cb[:, CB_BD64:CB_BD64 + 128], sqb[i2][:], True, True, [e_cb, e_sqb[i2]], [pe[b2]])
                    k.act(rsb[i2][:], ps[b2][:], AF.Ln, [e_cf], [pe[b2], e_rsb[i2]], bias=EPS)
                    k.act(rsb[i2][:], rsb[i2][:], AF.Exp, [], [e_rsb[i2]], scale=-0.5)
                    if which == 0:
                        k.stt(QT[:, c, tt * TT:(tt + 1) * TT], q32[i2][:], dv[:, l, 72:73], rsb[i2][:], ALU.mult, ALU.mult,
                              [e_q32[i2], e_rsb[i2], e_dv], [e_Q[c][tt]])
                    else:
                        k.stt(KT[:, c, tt * TT:(tt + 1) * TT], q32[i2][:], vec[:, l, 234:235], rsb[i2][:], ALU.mult, ALU.mult,
                              [e_q32[i2], e_rsb[i2], e_vec], [e_K[c][tt]])
            for tb in range(tt * 4, tt * 4 + 4):
                b = bank()
                for kc in range(8):
                    k.mm(ps[b][:], htl[tt % 2][:, kc, (tb % 4) * 128:(tb % 4 + 1) * 128], wv[:, kc, :], kc == 0, kc == 7, [e_wv, e_htl[tt % 2]], [pe[b]])
                if tb % 2:
                    k.act(VV[:, tb, :], ps[b][:], AF.Identity, [], [pe[b], e_V[tb]])
                else:
                    k.copy("dve", VV[:, tb, :], ps[b][:], [], [pe[b], e_V[tb]])
        P.pop(); P.barrier()
        nmk = P.sb([128, 2048], BF16); e_nmk = Ent()
        k.dma("sp", nmk[:], cb_d[:, CB_NMASK:CB_NMASK + 2048], [], [e_nmk])
        esb = P.sb([128, 2, TT], F32); e_esb = Ent()
        lkb = [P.sb([128, 2, TT], BF16) for _ in range(2)]; e_lk = [Ent(), Ent()]
        Ls = P.sb([128, 2, TT], F32); e_Ls = Ent()
        Lhi = [P.sb([128, 2, TT], BF16) for _ in range(2)]; e_Lhi = [Ent(), Ent()]
        ATb = [P.sb([128, 2, TT], BF16) for _ in range(2)]; e_AT = [Ent(), Ent()]
        osb = [P.sb([128, TT], BF16) for _ in range(2)]; e_osb = [Ent(), Ent()]
        OB = 6
        wgt = P.sb([128, 8, 3072], BF16); e_wgt = [Ent() for _ in range(3)]
        for q3 in range(3):
            k.dma("pool", wgt[:, :, q3 * 1024:(q3 + 1) * 1024], wview(w_in[l], C_GL + q3 * 1024, C_GL + (q3 + 1) * 1024), [], [e_wgt[q3]])
        hg = [P.sb([128, 8, TT], BF16) for _ in range(2)]; e_hg = [Ent(), Ent()]
        gpre = [P.sb([128, TT], F32) for _ in range(3)]; e_gpre = [Ent() for _ in range(3)]
        gstate = {"n": 0}

        gate_mms = [(tt_, ch_, kc_) for tt_ in range(NT) for ch_ in range(24) for kc_ in range(8)]

        def gate_mm():
            if gstate["n"] >= len(gate_mms):
                return
            n_ = gstate["n"]; gstate["n"] += 1
            tt_, ch_, kc = gate_mms[n_]
            hb = tt_ % 2
            if ch_ == 0 and kc == 0:
                k.dma("sp", hg[hb][:], HT[tt_], [e_HT], [e_hg[hb]])
            k.mm(ps[7][:], wgt[:, kc, ch_ * 128:(ch_ + 1) * 128], hg[hb][:, kc, :], kc == 0, kc == 7,
                 [e_wgt[ch_ // 8], e_hg[hb]], [pe[7]])
            if kc == 7:
                gpend.append((n_ // 8, tt_, ch_))

        gpend = []

        def gate_flush():
            while gpend:
                u_, tt_, ch_ = gpend.pop(0)
                i3_ = u_ % 3
                k.copy("dve", gpre[i3_][:], ps[7][:], [], [pe[7], e_gpre[i3_]])
                k.dma("sp", GP[tt_][:, ch_, :], gpre[i3_][:], [e_gpre[i3_]], [e_GP[tt_]])

        def make_group(g, c, base):
            nkb = 4 * (g + 1)
            kbs = list(range(nkb - 1, -1, -1))

            def c0of(i):
                r = kbs[i] - 4 * g
                return 128 * r if r > 0 else 0

            def Zmm(i):
                kb = kbs[i]; r = kb - 4 * g; j3 = (base + i) % 3; c0 = c0of(i)
                for hh in range(2):
                    pb = 64 * hh; zb = 2 * j3 + hh
                    k.mm(ps[zb][:, c0:], KT[pb:pb + 64, c, kb * 128:(kb + 1) * 128], QT[pb:pb + 64, c, g * TT + c0:(g + 1) * TT],
                         True, r < 0, [e_K[c][kb // 4], e_Q[c][g]], [pe[zb]])
                    if r >= 0:
                        k.mm(ps[zb][:, c0:], cb[:, CB_ID:CB_ID + 128], nmk[:, r * 512 + c0:(r + 1) * 512],
                             False, True, [e_cb, e_nmk], [pe[zb]])

            def A_act(i):
                j = (base + i) % 2; j3 = (base + i) % 3; c0 = c0of(i)
                k.act(esb[:, :, c0:], psall[:, 2 * j3:2 * j3 + 2, c0:], AF.Exp, [], [pe[2 * j3], pe[2 * j3 + 1], e_esb])
                k.act(lkb[j][:, :, c0:], esb[:, :, c0:], AF.Ln, [e_esb, e_cf], [e_lk[j]], bias=ONE)

            def tail(i):
                if i == 0:
                    k.memset("pool", Ls[:], 0.0, [e_Ls])
                if i >= nkb - 1:
                    return
                j = (base + i) % 2; jn = (base + i + 1) % 2; c0 = c0of(i); c0n = c0of(i + 1)
                k.tt("dve", Ls[:, :, c0:], Ls[:, :, c0:], lkb[j][:, :, c0:], ALU.add, [e_lk[j]], [e_Ls])
                k.copy("dve", Lhi[jn][:, :, c0n:], Ls[:, :, c0n:], [e_Ls], [e_Lhi[jn]])

            def B_tri(i):
                j = (base + i) % 2; j3 = (base + i) % 3; c0 = c0of(i)
                for hh in range(2):
                    zb = 2 * j3 + hh
                    k.mm(ps[zb][:, c0:], cb[:, CB_NTRI:CB_NTRI + 128], lkb[j][:, hh, c0:], False, i == 0, [e_cb, e_lk[j]], [pe[zb]], skip=True)
                    if i > 0:
                        k.mm(ps[zb][:, c0:], cb[:, CB_NONE:CB_NONE + 128], Lhi[j][:, hh, c0:], False, True, [e_cb, e_Lhi[j]], [pe[zb]], skip=True)

            def B_exp(i):
                j = (base + i) % 2; j3 = (base + i) % 3; c0 = c0of(i)
                k.act(ATb[j][:, :, c0:], psall[:, 2 * j3:2 * j3 + 2, c0:], AF.Exp, [], [pe[2 * j3], pe[2 * j3 + 1], e_AT[j]])

            def AV(i):
                kb = kbs[i]; j = (base + i) % 2; c0 = c0of(i)
                for hh in range(2):
                    h = 2 * c + hh
                    k.mm(ps[OB][64 * hh:64 * hh + 64, c0:], VV[:, kb, h * 64:(h + 1) * 64], ATb[j][:, hh, c0:], i == 0, i == nkb - 1,
                         [e_V[kb], e_AT[j]], [pe[OB]], skip=True)
                if i == nkb - 1:
                    ob_ = (g * 4 + c) % 2
                    k.copy("dve", osb[ob_][:], ps[OB][:], [], [pe[OB], e_osb[ob_]])
                    k.dma("sp", OS[g, :, c, :], osb[ob_][:], [e_osb[ob_]], [e_OS])

            return dict(nkb=nkb, Zmm=Zmm, A_act=A_act, tail=tail, B_tri=B_tri, B_exp=B_exp, AV=AV)

        steps = []
        base = 0
        for g in range(NT):
            for c in range(4):
                G_ = make_group(g, c, base)
                for i in range(G_["nkb"]):
                    steps.append((G_, i))
                base += G_["nkb"]
        G0, i0 = steps[0]
        G0["Zmm"](i0); G0["A_act"](i0); G0["tail"](i0)
        gacc = 0.0
        gper = len(gate_mms) / float(len(steps))
        for kk_ in range(len(steps)):
            Gc, i = steps[kk_]
            nxt = steps[kk_ + 1] if kk_ + 1 < len(steps) else None
            if nxt:
                nxt[0]["Zmm"](nxt[1])
            Gc["B_tri"](i)
            if kk_ >= 1:
                Gp, ip = steps[kk_ - 1]
                Gp["AV"](ip)
            gacc += gper
            while gacc >= 1.0 - 1e-9:
                if gpend and gate_mms[gstate["n"]][2] == 0 if gstate["n"] < len(gate_mms) else False:
                    gate_flush()
                gate_mm()
                gacc -= 1.0
            if nxt:
                nxt[0]["A_act"](nxt[1])
            Gc["B_exp"](i)
            if nxt:
                nxt[0]["tail"](nxt[1])
            gate_flush()
        Gl, il = steps[-1]
        Gl["AV"](il)
        while gstate["n"] < len(gate_mms):
            if gpend and gate_mms[gstate["n"]][2] == 0:
                gate_flush()
            gate_mm()
        gate_flush()
        P.pop(); P.barrier()
        if f"OS{l}" in dbg_d:
            dbg_dump(f"OS{l}", OS, e_OS, 4, True)
        P.push()
        wh = P.sb([128, 8, 2048], BF16); e_wh = Ent()
        k.dma("pool", wh[:], wview(w_in[l], C_HQ, C_HQ + 2048), [], [e_wh])
        V64 = [P.sb([128, 4, 512], BF16) for _ in range(2)]; e_V64 = [Ent(), Ent()]
        qs = P.sb([128, 4, TT], F32); e_qs = Ent()
        sg = P.sb([128, 4, TT], F32); e_sg = Ent()
        kk = P.sb([128, 4, TT], F32); e_kk = Ent()
        lf = P.sb([128, 4, TT], F32); e_lf = Ent()
        bb = P.sb([128, 4, TT], F32); e_bb = Ent()
        bp = P.sb([128, 4, TT], F32); e_bp = Ent()
        eb = sg; e_eb = e_sg
        enb = lf; e_enb = e_lf
        qt = [P.sb([128, 4, TT], BF16) for _ in range(2)]; e_qt = [Ent(), Ent()]
        kt = P.sb([128, 4, TT], BF16); e_kt = Ent()
        gs = [P.sb([128, 4, TT], BF16) for _ in range(2)]; e_gs = [Ent(), Ent()]
        d1 = [P.sb([128, 4, 8], F32) for _ in range(2)]; d2 = [P.sb([128, 4, 8], F32) for _ in range(2)]
        d3 = [P.sb([128, 4, 8], F32) for _ in range(2)]; e_d = [Ent(), Ent()]
        st32 = P.sb([128, 4, 128], F32); e_st = Ent()
        scA = [P.sb([128, 8, 4, 64], BF16) for _ in range(2)]; e_scA = [[Ent() for _ in range(8)] for _ in range(2)]
        ktok = [P.sb([128, 4, 128], BF16) for _ in range(2)]; e_ktok = [Ent(), Ent()]
        tm2A = [P.sb([128, 8, 4, 128], F32) for _ in range(2)]; e_tm2A = [[Ent() for _ in range(8)] for _ in range(2)]
        stbfA = P.sb([128, 8, 4, 128], BF16); e_stbfA = [Ent() for _ in range(8)]
        oT = P.sb([128, 4, TT], F32); e_oT = Ent()
        ors = P.sb([128, TT], F32); e_ors = Ent()
        ot2 = bb; e_ot2 = e_bb
        oh = P.sb([128, 4, TT], BF16); e_oh = Ent()
        k.memset("pool", st32[:], 0.0, [e_st])
        k.memset("dve", ps[0][:], 0.0, [pe[0]])
        def cmask_bc(po):
            cm = cb[po:po + 64, CB_CMASK:CB_CMASK + 64]
            return bass.AP(cm.tensor, cm.offset, [list(cm.ap[0]), [0, 4], [1, 64]])

        def bc128(d, n):
            v = d[:, :, n:n + 1]
            return bass.AP(v.tensor, v.offset, [list(v.ap[0]), list(v.ap[1]), [0, 128]])

        htl = [P.sb([128, 8, TT], BF16) for _ in range(2)]; e_htl = [Ent(), Ent()]

        def P1(tt):
            p = tt % 2
            hb, e_hb = htl[p], e_htl[p]
            for hh in range(4):
                b = bank((4, 5, 6, 7))
                for kc in range(8):
                    k.mm(ps[b][:], wh[:, kc, hh * 128:(hh + 1) * 128], hb[:, kc, :], kc == 0, kc == 7, [e_wh, e_hb], [pe[b]])
                k.act(qs[:, hh, :], ps[b][:], AF.Silu, [], [pe[b], e_qs])
                b = bank((4, 5, 6, 7))
                for kc in range(8):
                    k.mm(ps[b][:], wh[:, kc, 512 + hh * 128:512 + (hh + 1) * 128], hb[:, kc, :], kc == 0, kc == 7, [e_wh, e_hb], [pe[b]])
                k.act(sg[:, hh, :], ps[b][:], AF.Exp, [], [pe[b], e_sg])
                k.act(lf[:, hh, :], sg[:, hh, :], AF.Ln, [e_sg, e_dv], [e_lf], bias=dv[:, l, 76 + hh:77 + hh])
                k.act(bp[:, hh, :], sg[:, hh, :], AF.Ln, [e_sg, e_cf], [e_bp], bias=ONE)
                k.act(sg[:, hh, :], bp[:, hh, :], AF.Exp, [e_bp], [e_sg], scale=-1.0)
                k.ts("dve", kk[:, hh, :], sg[:, hh, :], dv[:, l, 64 + hh:65 + hh], ALU.mult, [e_sg, e_dv], [e_kk])
                k.tt("pool", lf[:, hh, :], lf[:, hh, :], bp[:, hh, :], ALU.subtract, [e_bp], [e_lf])
                b = bank((4, 5, 6, 7))
                for kc in range(8):
                    k.mm(ps[b][:], wh[:, kc, 1536 + hh * 128:1536 + (hh + 1) * 128], hb[:, kc, :], kc == 0, kc == 7, [e_wh, e_hb], [pe[b]])
                k.act(gs[p][:, hh, :], ps[b][:], AF.Silu, [], [pe[b], e_gs[p]])
            for blk in range(4):
                b = bank((4, 5, 6, 7))
                for kc in range(8):
                    k.mm(ps[b][:], hb[:, kc, blk * 128:(blk + 1) * 128], wh[:, kc, 1024:1536], kc == 0, kc == 7, [e_wh, e_hb], [pe[b]])
                k.copy("dve", V64[p][:, blk, :], ps[b][:], [], [pe[b], e_V64[p]])
            for hh in range(4):
                P.add("dve", (lambda hh: lambda g_: g_.tensor_tensor_scan(out=bb[:, hh, :], data0=cf[:, 0:512], data1=lf[:, hh, :],
                                                                        initial=0.0, op0=ALU.mult, op1=ALU.add))(hh),
                      [e_cf, e_lf], [e_bb])
            bbv = bb[:].rearrange("p h (n c) -> p (h n) c", c=64)
            bpv = bp[:].rearrange("p h (n c) -> p (h n) c", c=64)
            k.tt("dve", bpv, bbv, bcast_last(bbv[:, :, 31:32], 64), ALU.subtract, [e_bb], [e_bp])
            k.act(d1[p][:].rearrange("p h n -> p (h n)"), bbv[:, :, 63], AF.Exp, [e_bb], [e_d[p]])
            k.act(d2[p][:].rearrange("p h n -> p (h n)"), bpv[:, :, 63], AF.Exp, [e_bp], [e_d[p]])
            k.act(d3[p][:].rearrange("p h n -> p (h n)"), bbv[:, :, 31], AF.Exp, [e_bb], [e_d[p]])
            k.act(eb[:], bp[:], AF.Exp, [e_bp], [e_eb])
            k.act(enb[:], bp[:], AF.Exp, [e_bp], [e_enb], scale=-1.0)
            k.tt("dve", qt[p][:], qs[:], eb[:], ALU.mult, [e_qs, e_eb], [e_qt[p]])
            k.tt("pool", kt[:], kk[:], enb[:], ALU.mult, [e_kk, e_enb], [e_kt])

        def P2(tt):
            p = tt % 2
            bS = 0
            for n in range(8):
                csl = slice(n * 64, (n + 1) * 64)
                bK, bU = (1, 2) if n % 2 == 0 else (3, 4)
                po = 64 * (n % 2); blk = n // 2
                for hh in range(4):
                    k.mm(ps[bS][po:po + 64, hh * 64 + 32:(hh + 1) * 64], kt[:, hh, csl], qt[p][:, hh, n * 64 + 32:(n + 1) * 64], True, True,
                         [e_kt, e_qt[p]], [pe[bS]])
                    k.mm(ps[bS][po:po + 32, hh * 64:hh * 64 + 32], kt[:, hh, n * 64:n * 64 + 32], qt[p][:, hh, n * 64:n * 64 + 32], True, True,
                         [e_kt, e_qt[p]], [pe[bS]])
                for hh in range(4):
                    k.mm(ps[bK][po:po + 64, hh * 128:(hh + 1) * 128], kt[:, hh, csl], cb[:, CB_ID:CB_ID + 128], True, True, [e_kt, e_cb], [pe[bK]])
                k.tt("dve", scA[p][po:po + 64, n, :, :], ps[bS][po:po + 64, 0:256].rearrange("p (h c) -> p h c", c=64), cmask_bc(po), ALU.mult,
                     [e_cb], [pe[bS], e_scA[p][n]])
                k.act(ktok[n % 2][po:po + 64], ps[bK][po:po + 64, :].rearrange("p (h c) -> p h c", c=128), AF.Identity, [], [pe[bK], e_ktok[n % 2]])
                for hh in range(4):
                    k.mm(ps[bU][:, hh * 128:(hh + 1) * 128], ktok[n % 2][po:po + 64, hh, :], V64[p][po:po + 64, blk, hh * 128:(hh + 1) * 128], True, True,
                         [e_ktok[n % 2], e_V64[p]], [pe[bU]])
                k.tt("dve", tm2A[p][:, n, :, :], ps[bU][:].rearrange("p (h c) -> p h c", c=128), bc128(d2[p], n), ALU.mult, [e_d[p]],
                     [pe[bU], e_tm2A[p][n]])

        def S3(tt):
            p = tt % 2
            for n in range(8):
                k.tt("dve", stbfA[:, n, :, :], st32[:], bc128(d3[p], n), ALU.mult, [e_st, e_d[p]], [e_stbfA[n]])
                for hh in range(4):
                    k.stt(st32[:, hh, :], st32[:, hh, :], d1[p][:, hh, n:n + 1], tm2A[p][:, n, hh, :], ALU.mult, ALU.add,
                          [e_d[p], e_tm2A[p][n]], [e_st])

        def Q4(tt):
            p = tt % 2
            for n2 in range(4):
                bO = 5 + n2 % 2
                for n in (2 * n2, 2 * n2 + 1):
                    csl = slice(n * 64, (n + 1) * 64)
                    for hh in range(4):
                        oc0 = (n % 2) * 256 + hh * 64
                        po = 64 * (n % 2); blk = n // 2
                        k.mm(ps[bO][:, oc0:oc0 + 64], V64[p][po:po + 64, blk, hh * 128:(hh + 1) * 128], scA[p][po:po + 64, n, hh, :], True, False,
                             [e_V64[p], e_scA[p][n]], [pe[bO]])
                        k.mm(ps[bO][:, oc0:oc0 + 64], stbfA[:, n, hh, :], qt[p][:, hh, csl], False, True, [e_stbfA[n], e_qt[p]], [pe[bO]])
                k.act(oT[:, :, n2 * 128:(n2 + 1) * 128].rearrange("p h (n c) -> p n h c", c=64),
                      ps[bO][:].rearrange("p (n h c) -> p n h c", n=2, h=4), AF.Identity, [], [pe[bO], e_oT])
            osq = qt[p]; e_osq = e_qt[p]
            k.act(osq[:], oT[:], AF.Square, [e_oT], [e_osq])
            for hh in range(4):
                b = bank((4, 5, 6, 7))
                k.mm(ps[b][:], cb[:, CB_O128:CB_O128 + 128], osq[:, hh, :], True, True, [e_cb, e_osq], [pe[b]])
                k.act(ors[:], ps[b][:], AF.Ln, [e_cf], [pe[b], e_ors], bias=EPS)
                k.act(ors[:], ors[:], AF.Exp, [], [e_ors], scale=-0.5)
                k.stt(ot2[:, hh, :], oT[:, hh, :], vec[:, l, 232:233], ors[:], ALU.mult, ALU.mult, [e_oT, e_ors, e_vec], [e_ot2])
                k.tt("pool", oh[:, hh, :], ot2[:, hh, :], gs[p][:, hh, :], ALU.mult, [e_ot2, e_gs[p]], [e_oh])
            k.dma("sp", OH[tt], oh[:], [e_oh], [e_OH])

        k.dma("sp", htl[0][:], HT[0], [e_HT], [e_htl[0]])
        if NT > 1:
            k.dma("sp", htl[1][:], HT[1], [e_HT], [e_htl[1]])
        P1(0)
        P2(0)
        for tt in range(NT):
            S3(tt)
            if tt + 1 < NT:
                P1(tt + 1)
            if tt + 2 < NT:
                k.dma("sp", htl[tt % 2][:], HT[tt + 2], [e_HT], [e_htl[tt % 2]])
            Q4(tt)
            if tt + 1 < NT:
                P2(tt + 1)
        P.pop(); P.barrier()
        if f"OH{l}" in dbg_d:
            dbg_dump(f"OH{l}", OH, e_OH, 4, True)
        P.push()
        wpr = [P.sb([128, 4, D], BF16) for _ in range(3)]; e_wpr = [Ent() for _ in range(3)]
        wo = P.sb([128, 8, D], BF16); e_wo = Ent()
        for q3, wsrc in enumerate((w_cp, w_hp, w_sp)):
            k.dma("pool", wpr[q3][:], wview(wsrc[l], 0, D), [], [e_wpr[q3]])
        k.dma("pool", wo[:], wview(w_out[l], 0, D), [], [e_wo])
        obr = [[P.sb([128, 4, TT], BF16) for _ in range(3)] for _ in range(2)]
        e_obr = [[Ent() for _ in range(3)] for _ in range(2)]
        gpl = [P.sb([128, 3, TT], F32) for _ in range(4)]; e_gpl = [Ent() for _ in range(4)]
        gsb = [P.sb([128, TT], F32) for _ in range(3)]; e_gsb = [Ent() for _ in range(3)]
        xt = [P.sb([128, 8, TT], F32) for _ in range(2)]; e_xt = [Ent(), Ent()]
        mg = [P.sb([128, 8, TT], BF16) for _ in range(2)]; e_mgc = [[Ent() for _ in range(8)] for _ in range(2)]
        ma = [[P.sb([128, TT], F32) for _ in range(3)] for _ in range(2)]; e_ma = [[Ent() for _ in range(3)] for _ in range(2)]
        def loadE(tt_):
            b_ = tt_ % 2
            for i3, (src, esrc) in enumerate(((OC, e_OC), (OH, e_OH), (OS, e_OS))):
                k.dma("sp", obr[b_][i3][:], src[tt_], [esrc], [e_obr[b_][i3]])
            k.dma("sp", xt[b_][:], x_src[tt_], [e_xsrc], [e_xt[b_]])

        gq = [(tt_, c_) for tt_ in range(NT) for c_ in range(8)]

        def loadG(qi):
            if qi < len(gq):
                tt_, c_ = gq[qi]
                k.dma("sp", gpl[qi % 4][:], GP[tt_].rearrange("p (i c) t -> p i c t", c=8)[:, :, c_, :], [e_GP[tt_]], [e_gpl[qi % 4]])

        def Yst(tt):
            b2 = tt % 2
            for c in range(8):
                bs = [bank(), bank(), bank()]
                loadG(tt * 8 + c + 3)
                for i3 in range(3):
                    ch = i3 * 8 + c
                    k.act(gsb[i3][:], gpl[(tt * 8 + c) % 4][:, i3, :], AF.Sigmoid, [e_vec, e_gpl[(tt * 8 + c) % 4]], [e_gsb[i3]], bias=vec[:, l, 64 + ch:65 + ch])
                    for kc in range(4):
                        k.mm(ps[bs[i3]][:], wpr[i3][:, kc, c * 128:(c + 1) * 128], obr[b2][i3][:, kc, :], kc == 0, kc == 3,
                             [e_wpr[i3], e_obr[b2][i3]], [pe[bs[i3]]])
                mq = ma[c % 2]; e_mq = e_ma[c % 2]
                for i3 in range(3):
                    k.tt("dve", mq[i3][:], ps[bs[i3]][:], gsb[i3][:], ALU.mult, [e_gsb[i3]], [pe[bs[i3]], e_mq[i3]])
                k.tt("dve", mq[0][:], mq[0][:], mq[1][:], ALU.add, [e_mq[1]], [e_mq[0]])
                k.tt("pool", mg[b2][:, c, :], mq[0][:], mq[2][:], ALU.add, [e_mq[0], e_mq[2]], [e_mgc[b2][c]])

        def Wst(tt):
            b2 = tt % 2
            for c in range(8):
                b = bank()
                for kc in range(8):
                    k.mm(ps[b][:], wo[:, kc, c * 128:(c + 1) * 128], mg[b2][:, kc, :], kc == 0, kc == 7, [e_wo, e_mgc[b2][kc]], [pe[b]])
                k.stt(xt[b2][:, c, :], ps[b][:], dv[:, l, 16 + c:17 + c], xt[b2][:, c, :], ALU.mult, ALU.add, [e_dv], [pe[b], e_xt[b2]])
            k.dma("sp", xm_s[tt], xt[b2][:], [e_xt[b2]], [e_xm])

        loadE(0)
        if NT > 1:
            loadE(1)
        loadG(0); loadG(1); loadG(2)
        Yst(0)
        for tt in range(NT):
            if tt + 1 < NT:
                Yst(tt + 1)
            Wst(tt)
            if tt + 2 < NT:
                loadE(tt + 2)
        P.pop(); P.barrier()
        if f"xm{l}" in dbg_d:
            dbg_dump(f"xm{l}", xm_s, e_xm, 8, False)

        P.push()
        w1 = P.sb([128, 8, 4 * D], BF16); e_w1q = [Ent() for _ in range(4)]
        w2 = P.sb([128, 32, D], BF16); e_w2q = [Ent() for _ in range(4)]
        for q4 in range(4):
            k.dma("pool", w1[:, :, q4 * 1024:(q4 + 1) * 1024], wview(w1_d[l], q4 * 1024, (q4 + 1) * 1024), [], [e_w1q[q4]])
        for q4 in range(4):
            k.dma("pool", w2[:, q4 * 8:(q4 + 1) * 8, :], w2_d[l].rearrange("(kc p) n -> p kc n", p=128)[:, q4 * 8:(q4 + 1) * 8, :], [], [e_w2q[q4]])
        xt = P.sb([128, 8, TT], F32); e_xt = Ent()
        sqr = [P.sb([128, TT], BF16) for _ in range(2)]; e_sqr = [Ent(), Ent()]
        rs = P.sb([128, TT], F32); e_rs = Ent()
        tmp = [P.sb([128, TT], F32) for _ in range(2)]; e_tmp = [Ent(), Ent()]
        h2 = P.sb([128, 8, TT], BF16); e_h2 = Ent()
        hid = P.sb([128, 32, TT], BF16); e_hidj = [Ent() for _ in range(32)]
        rl = P.sb([128, TT], F32); e_rl = Ent()
        xr = [P.sb([128, TT], F32) for _ in range(2)]; e_xr = [Ent(), Ent()]

        def normF():
            norm_mod(xt, e_xt, lambda kc: dv[:, l, 56 + kc:57 + kc], lambda kc: dv[:, l, 24 + kc:25 + kc],
                     lambda kc: h2[:, kc, :], e_h2, sqr, e_sqr, rs, e_rs, tmp, e_tmp)

        k.dma("sp", xt[:], xm_s[0], [e_xm], [e_xt])
        normF()
        if NT > 1:
            k.dma("sp", xt[:], xm_s[1], [e_xm], [e_xt])
        for tt in range(NT):
            for j in range(32):
                b = bank()
                for kc in range(8):
                    k.mm(ps[b][:], w1[:, kc, j * 128:(j + 1) * 128], h2[:, kc, :], kc == 0, kc == 7, [e_w1q[j // 8], e_h2], [pe[b]])
                k.act(rl[:], ps[b][:], AF.Relu, [], [pe[b], e_rl])
                k.tt("pool" if j % 2 else "dve", hid[:, j, :], rl[:], rl[:], ALU.mult, [e_rl], [e_hidj[j]])
            k.dma("sp", xr[0][:], xm_s[tt][:, 0, :], [e_xm], [e_xr[0]])
            if tt + 1 < NT:
                normF()
                if tt + 2 < NT:
                    k.dma("sp", xt[:], xm_s[tt + 2], [e_xm], [e_xt])
            for c in range(8):
                if c + 1 < 8:
                    k.dma("sp", xr[(c + 1) % 2][:], xm_s[tt][:, c + 1, :], [e_xm], [e_xr[(c + 1) % 2]])
                b = bank()
                for j in range(32):
                    k.mm(ps[b][:], w2[:, j, c * 128:(c + 1) * 128], hid[:, j, :], j == 0, j == 31, [e_w2q[j // 8], e_hidj[j]], [pe[b]])
                k.stt(xr[c % 2][:], ps[b][:], dv[:, l, 40 + c:41 + c], xr[c % 2][:], ALU.mult, ALU.add, [e_dv], [pe[b], e_xr[c % 2]])
                k.dma("sp", x_dst[tt][:, c, :], xr[c % 2][:], [e_xr[c % 2]], [e_xs])
        P.pop(); P.barrier()
    P.barrier()
    P.emit()
    global LASTP
    LASTP = P
    return nc


def make_inputs(inp, B0, S, L, shared=None):
    if shared is None:
        cbv, cfv = host_consts()
        shared = {
            "cbf": cbv, "cf32": cfv,
            "vecs": np.ascontiguousarray(np.stack([host_vecs(inp, l) for l in range(L)], axis=1)),
        }
        for n in ("mod_w", "w_in", "w_conv_proj", "w_hgrn_proj", "w_sb_proj", "w_out", "mlp_w1", "mlp_w2"):
            shared[n] = np.ascontiguousarray(np.asarray(inp[n], np.float32)[:L])
    m = dict(shared)
    m["xT"] = np.ascontiguousarray(np.asarray(inp["x"][B0], np.float32)[:S].reshape(S // TT, TT, 8, 128).transpose(0, 3, 2, 1))
    m["cc"] = np.ascontiguousarray(np.asarray(inp["c"][B0], np.float32).reshape(8, 128).T)
    return m


_CACHE = {}


def untile(o):
    nt = o.shape[0]
    return np.ascontiguousarray(o.transpose(0, 3, 2, 1).reshape(nt * TT, 1024))


def kernel(**inputs):
    B, S, _ = inputs["x"].shape
    L = int(np.asarray(inputs["mod_w"]).shape[0])
    key = (S, L)
    if key not in _CACHE:
        _CACHE[key] = build(S, L)
    nc = _CACHE[key]
    inp = {k_: np.asarray(v) for k_, v in inputs.items()}
    first = make_inputs(inp, 0, S, L)
    shared = {k_: v for k_, v in first.items() if k_ not in ("xT", "cc")}
    in_maps = [first] + [make_inputs(inp, b, S, L, shared=shared) for b in range(1, B)]
    res = run_bass_kernel_spmd(nc, in_maps, core_ids=list(range(B)))
    out = np.stack([untile(np.asarray(r["outT"], np.float32)) for r in res.results], axis=0)
    return out.astype(np.float32)
```

```python
import numpy as np
import ml_dtypes
import concourse.bass as bass
import concourse.mybir as mybir
from concourse.bass_utils import run_bass_kernel_spmd

F32 = mybir.dt.float32
BF16 = mybir.dt.bfloat16
AF = mybir.ActivationFunctionType
ALU = mybir.AluOpType
NPBF = ml_dtypes.bfloat16

ENGS = ("pe", "act", "dve", "pool", "sp")
EPOCH = 24000


class Ent:
    __slots__ = ("w", "r")

    def __init__(self):
        self.w = None
        self.r = {}


class Op:
    __slots__ = ("fn", "deps", "dma", "sig", "slot", "seq")

    def __init__(self, fn, deps, dma, slot=0, seq=0):
        self.fn = fn
        self.deps = deps
        self.dma = dma
        self.sig = dma
        self.slot = slot
        self.seq = seq


NSLOT = 8


class Prog:
    def __init__(self, nc):
        self.nc = nc
        self.ops = {e: [] for e in ENGS}
        self.dmas = {e: [] for e in ENGS}
        self.sb_off = 16640
        self.sb_mark = []
        self.n_t = 0

    def sb(self, shape, dtype, name=None):
        esz = 2 if dtype == BF16 else 4
        n = 1
        for s in shape[1:]:
            n *= s
        nbytes = (n * esz + 63) // 64 * 64
        off = self.sb_off
        self.sb_off += nbytes
        assert self.sb_off <= 229376, f"SBUF overflow {self.sb_off}"
        self.n_t += 1
        return self.nc.alloc_sbuf_tensor_at(f"{name or 't'}{self.n_t}", list(shape), dtype, offset=off)

    def push(self):
        self.sb_mark.append(self.sb_off)

    def pop(self):
        self.sb_off = self.sb_mark.pop()

    def add(self, eng, fn, reads=(), writes=(), dma=False):
        idx = len(self.ops[eng])
        tok = (eng, idx)
        deps = set()
        for e in reads:
            if e.w is not None:
                deps.add(e.w)
        for e in writes:
            if e.w is not None:
                deps.add(e.w)
            for t in e.r.values():
                deps.add(t)
        deps.discard(tok)
        slot = seq = 0
        if dma:
            seq = len(self.dmas[eng])
            slot = seq % NSLOT
            if seq >= NSLOT:
                deps.add((eng, self.dmas[eng][seq - NSLOT]))
            self.dmas[eng].append(idx)
        rk = (eng, "d", slot) if dma else (eng, "c")
        for e in reads:
            e.r[rk] = tok
        for e in writes:
            e.w = tok
            e.r = {}
        self.ops[eng].append(Op(fn, deps, dma, slot, seq))
        return tok

    def barrier(self):
        last = []
        for e in ENGS:
            seen_c = False
            seen_d = set()
            for i in range(len(self.ops[e]) - 1, -1, -1):
                op = self.ops[e][i]
                if op.fn is None:
                    continue
                if op.dma:
                    if op.slot not in seen_d:
                        last.append((e, i))
                        seen_d.add(op.slot)
                elif not seen_c:
                    last.append((e, i))
                    seen_c = True
                if seen_c and len(seen_d) == NSLOT:
                    break
        for e in ENGS:
            self.ops[e].append(Op(None, set(last), False))
            self.ops[e][-1].sig = False

    def emit(self):
        nc = self.nc
        ops = self.ops
        for e in ENGS:
            for i, op in enumerate(ops[e]):
                for (e2, j) in op.deps:
                    if e2 == "pe" and e == "pe":
                        continue
                    ops[e2][j].sig = True
        tokval = {}
        sems = {}
        for e in ENGS:
            cnt = 0
            for i, op in enumerate(ops[e]):
                if op.fn is None or not op.sig:
                    continue
                if op.dma:
                    n = op.seq // NSLOT
                    per = EPOCH // 16
                    k = (e, "d", op.slot, n // per)
                    v = (n % per + 1) * 16
                else:
                    k = (e, "c", 0, cnt // EPOCH)
                    v = cnt % EPOCH + 1
                    cnt += 1
                tokval[(e, i)] = (k, v)
                if k not in sems:
                    sems[k] = nc.alloc_semaphore("s_" + "_".join(str(x) for x in k))
        self.n_sems = len(sems)

        def run(e, eng):
            waited = {}
            for i, op in enumerate(ops[e]):
                need = {}
                for (e2, j) in op.deps:
                    if e2 == "pe" and e == "pe":
                        continue
                    k, v = tokval[(e2, j)]
                    if need.get(k, 0) < v:
                        need[k] = v
                for k, v in need.items():
                    if waited.get(k, 0) >= v:
                        continue
                    if any(kk[:3] == k[:3] and kk[3] > k[3] for kk in waited):
                        continue
                    eng.wait_ge(sems[k], v)
                    waited[k] = v
                if op.fn is None:
                    continue
                ins = op.fn(eng)
                if op.sig:
                    k, v = tokval[(e, i)]
                    ins.then_inc(sems[k], 16 if op.dma else 1)

        with nc.Block() as block:
            @block.tensor
            def _(eng):
                run("pe", eng)

            @block.scalar
            def _(eng):
                run("act", eng)

            @block.vector
            def _(eng):
                run("dve", eng)

            @block.gpsimd
            def _(eng):
                run("pool", eng)

            @block.sync
            def _(eng):
                run("sp", eng)


D = 1024
KC = 8
TT = 512
C_CA, C_CG, C_HQ, C_HF, C_HI, C_HG, C_SQ, C_SK, C_SV, C_GL = 0, 512, 1024, 1536, 2048, 2560, 3072, 3584, 4096, 4608
NV = 240
CB_ID, CB_OD, CB_O512, CB_BD64, CB_O128, CB_NTRI, CB_NONE, CB_CMASK, CB_NMASK, NCB = 0, 128, 256, 384, 512, 640, 768, 896, 960, 3008
NCF = 516
USE_LO = False


def host_consts():
    cb = np.zeros((128, NCB), np.float32)
    p = np.arange(128)
    cb[:, CB_ID:CB_ID + 128] = np.eye(128)
    cb[:, CB_OD:CB_OD + 128] = 1.0 / 1024
    cb[:, CB_O512:CB_O512 + 128] = 1.0 / 512
    cb[:, CB_BD64:CB_BD64 + 128] = (p[:, None] // 64 == p[None, :] // 64) / 64.0
    cb[:, CB_O128:CB_O128 + 128] = 1.0 / 128
    cb[:, CB_NTRI:CB_NTRI + 128] = -(p[:, None] >= p[None, :]).astype(np.float32)
    cb[:, CB_NONE:CB_NONE + 128] = -1.0
    t = np.arange(512)
    for r in range(4):
        valid = (128 * r + p[:, None]) < t[None, :]
        cb[:, CB_NMASK + r * 512:CB_NMASK + (r + 1) * 512] = np.where(valid, 0.0, -30000.0)
    cb[:64, CB_CMASK:CB_CMASK + 64] = (p[:64, None] <= p[None, :64]).astype(np.float32)
    cb[64:, CB_CMASK:CB_CMASK + 64] = cb[:64, CB_CMASK:CB_CMASK + 64]
    cf = np.ones((128, NCF), np.float32)
    cf[:, 0:512:64] = 0.0
    cf[:, 512] = 1e-6
    cf[:, 513] = 1.0
    cf[:, 514] = 0.0
    return cb.astype(NPBF), cf


def host_vecs(inp, l):
    v = np.zeros((128, NV), np.float32)
    col = lambda a: np.asarray(a, np.float32).reshape(-1, 128).T
    v[:, 0:48] = col(inp["mod_b"][l])
    v[:, 48:56] = col(inp["norm1_g"][l])
    v[:, 56:64] = col(inp["norm2_g"][l])
    v[:, 64:88] = col(inp["gate_b"][l])
    v[:, 88:92] = col(inp["conv_b"][l])
    v[:, 92:96] = col(inp["conv_ln_g"][l])
    v[:, 96:100] = col(inp["conv_ln_b"][l])
    cw = np.asarray(inp["conv_w"][l], np.float32)
    v[:, 100:224] = cw.T.reshape(4, 128, 31).transpose(1, 0, 2).reshape(128, 124)
    v[:, 224:228] = col(inp["hgrn_lb"][0])
    v[:, 228:232] = col(inp["hgrn_lb"][1])
    v[:, 232] = np.asarray(inp["hgrn_norm_g"][l], np.float32)
    v[:, 233] = np.tile(np.asarray(inp["sb_qn_g"][l], np.float32), 2)
    v[:, 234] = np.tile(np.asarray(inp["sb_kn_g"][l], np.float32), 2)
    return v


class K:
    def __init__(self, P):
        self.P = P

    def mm(self, out, lhsT, rhs, start, stop, reads, writes, skip=False):
        self.P.add("pe", lambda g: g.matmul(out, lhsT=lhsT, rhs=rhs, start=start, stop=stop, skip_group_check=skip),
                   reads, writes)

    def act(self, out, in_, func, reads, writes, scale=None, bias=None):
        kw = {}
        if scale is not None:
            kw["scale"] = scale
        if bias is not None:
            kw["bias"] = bias
        self.P.add("act", lambda g: g.activation(out=out, in_=in_, func=func, **kw), reads, writes)

    def tt(self, eng, out, in0, in1, op, reads, writes):
        self.P.add(eng, lambda g: g.tensor_tensor(out=out, in0=in0, in1=in1, op=op), reads, writes)

    def ts(self, eng, out, in0, s1, op0, reads, writes, s2=None, op1=None):
        if op1 is None:
            self.P.add(eng, lambda g: g.tensor_scalar(out=out, in0=in0, scalar1=s1, scalar2=None, op0=op0), reads, writes)
        else:
            self.P.add(eng, lambda g: g.tensor_scalar(out=out, in0=in0, scalar1=s1, scalar2=s2, op0=op0, op1=op1),
                       reads, writes)

    def stt(self, out, in0, scalar, in1, op0, op1, reads, writes):
        self.P.add("dve", lambda g: g.scalar_tensor_tensor(out=out, in0=in0, scalar=scalar, in1=in1, op0=op0, op1=op1),
                   reads, writes)

    def copy(self, eng, out, in_, reads, writes):
        self.P.add(eng, lambda g: g.tensor_copy(out=out, in_=in_), reads, writes)

    def memset(self, eng, ap, val, writes):
        self.P.add(eng, lambda g: g.memset(ap, val), (), writes)

    def recip(self, out, in_, reads, writes):
        self.P.add("dve", lambda g: g.reciprocal(out=out, in_=in_), reads, writes)

    def dma(self, q, out, in_, reads, writes):
        self.P.add(q, lambda g: g.dma_start(out=out, in_=in_), reads, writes, dma=True)


def bcast_last(ap3, n):
    a = ap3.ap
    return bass.AP(ap3.tensor, ap3.offset, [list(a[0]), list(a[1]), [0, n]])


def build(S, L=2, dbg=()):
    NT = S // TT
    nc = bass.Bass("TRN2", target_bir_lowering=False)
    P = Prog(nc)
    k = K(P)

    def dram(name, shape, dt, kind="Internal"):
        return nc.dram_tensor(name, list(shape), dt, kind=kind).ap()

    xT_in = dram("xT", [NT, 128, 8, TT], F32, "ExternalInput")
    outT = dram("outT", [NT, 128, 8, TT], F32, "ExternalOutput")
    cc_d = dram("cc", [128, 8], F32, "ExternalInput")
    cb_d = dram("cbf", [128, NCB], BF16, "ExternalInput")
    cf_d = dram("cf32", [128, NCF], F32, "ExternalInput")
    vec_d = dram("vecs", [128, L, NV], F32, "ExternalInput")
    mod_w = dram("mod_w", [L, D, 6 * D], F32, "ExternalInput")
    w_in = dram("w_in", [L, D, 7680], F32, "ExternalInput")
    w_cp = dram("w_conv_proj", [L, 512, D], F32, "ExternalInput")
    w_hp = dram("w_hgrn_proj", [L, 512, D], F32, "ExternalInput")
    w_sp = dram("w_sb_proj", [L, 512, D], F32, "ExternalInput")
    w_out = dram("w_out", [L, D, D], F32, "ExternalInput")
    w1_d = dram("mlp_w1", [L, D, 4 * D], F32, "ExternalInput")
    w2_d = dram("mlp_w2", [L, 4 * D, D], F32, "ExternalInput")
    xm_s = dram("xm_s", [NT, 128, 8, TT], F32)
    x_s = dram("x_s", [NT, 128, 8, TT], F32)
    HT = dram("HT", [NT, 128, 8, TT], BF16)
    OC = dram("OC", [NT, 128, 4, TT], BF16)
    OH = dram("OH", [NT, 128, 4, TT], BF16)
    OS = dram("OS", [NT, 128, 4, TT], BF16)
    GP = dram("GP", [NT, 128, 24, TT], F32)
    e_GP = [Ent() for _ in range(NT)]
    e_xm, e_xs, e_OC, e_OH, e_OS, e_HT = Ent(), Ent(), Ent(), Ent(), Ent(), Ent()
    dbg_d = {n: dram("dbg_" + n, shp, F32, "ExternalOutput") for n, shp in dbg}

    psall = nc.alloc_psum_tensor("psall", [128, 8, 512], F32)
    ps = [psall[:, i, :] for i in range(8)]
    pe = [Ent() for _ in range(8)]
    rot = {"i": 0}

    def bank(lst=(0, 1, 2, 3, 4, 5, 6, 7)):
        rot["i"] += 1
        return lst[rot["i"] % len(lst)]

    def wview(w2d, c0, c1):
        return w2d.rearrange("(kc p) n -> p kc n", p=128)[:, :, c0:c1]

    def fview(t4d, t0, t1):
        assert t0 % TT == 0 and t1 == t0 + TT
        return t4d[t0 // TT]

    def fview_rm(t2d, t0, t1):
        return t2d.rearrange("(kc p) t -> p kc t", p=128)[:, :, t0:t1]

    def dbg_dump(name, src4d, e_src, kcn, bf):
        P.push()
        for tt_ in range(NT):
            d32 = P.sb([128, kcn, TT], F32); e_d32 = Ent()
            if bf:
                d16 = P.sb([128, kcn, TT], BF16); e_d16 = Ent()
                k.dma("sp", d16[:], src4d[tt_], [e_src], [e_d16])
                k.copy("dve", d32[:], d16[:], [e_d16], [e_d32])
            else:
                k.dma("sp", d32[:], src4d[tt_], [e_src], [e_d32])
            k.dma("sp", fview_rm(dbg_d[name], tt_ * TT, (tt_ + 1) * TT), d32[:], [e_d32], [])
        P.pop(); P.barrier()


    cb = P.sb([128, CB_NMASK], BF16); e_cb = Ent()
    cf = P.sb([128, NCF], F32); e_cf = Ent()
    vec = P.sb([128, L, NV], F32); e_vec = Ent()
    dv = P.sb([128, L, 80], F32); e_dv = Ent()
    cc = P.sb([128, 8], F32); e_cc = Ent()
    cact = P.sb([128, 8], BF16); e_cact = Ent()
    k.dma("sp", cb[:], cb_d[:, 0:CB_NMASK], [], [e_cb])
    k.dma("sp", cf[:], cf_d, [], [e_cf])
    k.dma("sp", vec[:], vec_d, [], [e_vec])
    k.dma("sp", cc[:], cc_d, [], [e_cc])
    k.act(cact[:], cc[:], AF.Silu, [e_cc], [e_cact])
    EPS = cf[:, 512:513]
    ONE = cf[:, 513:514]
    def mod_dma(l_, grp, mw, e_mw):
        b = grp % 2
        k.dma("pool", mw[b][:], wview(mod_w[l_], grp * 1024, (grp + 1) * 1024), [], [e_mw[b]])

    def mod_group(l_, grp, mw, e_mw, dma=True):
        b = grp % 2
        if dma:
            mod_dma(l_, grp, mw, e_mw)
        for j in range(8):
            col = grp * 8 + j
            for kc in range(8):
                k.mm(ps[7][:, col:col + 1], mw[b][:, kc, j * 128:(j + 1) * 128], cact[:, kc:kc + 1], kc == 0, kc == 7,
                     [e_mw[b], e_cact], [pe[7]])

    def mod_finish(l_):
        k.tt("dve", dv[:, l_, 0:48], ps[7][:, 0:48], vec[:, l_, 0:48], ALU.add, [e_vec], [pe[7], e_dv])
        k.stt(dv[:, l_, 48:56], dv[:, l_, 8:16], 1.0, vec[:, l_, 48:56], ALU.add, ALU.mult, [e_vec], [e_dv])
        k.stt(dv[:, l_, 56:64], dv[:, l_, 32:40], 1.0, vec[:, l_, 56:64], ALU.add, ALU.mult, [e_vec], [e_dv])
        k.ts("dve", dv[:, l_, 72:73], vec[:, l_, 233:234], 0.125, ALU.mult, [e_vec], [e_dv])

    P.push()
    mw0 = [P.sb([128, 8, 1024], BF16) for _ in range(2)]
    e_mw0 = [Ent(), Ent()]
    for grp in range(6):
        mod_group(0, grp, mw0, e_mw0)
    mod_finish(0)
    k.memset("dve", dv[:, 0, 64:68], 1.0, [e_dv])
    k.memset("dve", dv[:, 0, 68:72], -1.0, [e_dv])
    k.memset("dve", dv[:, 0, 76:80], 0.0, [e_dv])
    if L > 1:
        k.tt("dve", dv[:, 1, 76:80], vec[:, 0, 224:228], vec[:, 0, 228:232], ALU.subtract, [e_vec], [e_dv])
        k.act(dv[:, 1, 64:68], dv[:, 1, 76:80], AF.Sigmoid, [], [e_dv])
        k.ts("dve", dv[:, 1, 68:72], dv[:, 1, 64:68], -1.0, ALU.mult, [], [e_dv])
        k.ts("dve", dv[:, 1, 76:80], dv[:, 1, 64:68], -1.0, ALU.mult, [], [e_dv], s2=1.0, op1=ALU.add)
    P.pop()
    P.barrier()
    if "dv" in dbg_d:
        k.dma("sp", dbg_d["dv"], dv[:], [e_dv], [])

    def norm_mod(xt, e_xt, acol, shcol, dst_fn, e_dst, sq, e_sq, rs, e_rs, tmp, e_tmp):
        b = bank()
        if isinstance(sq, list):
            for kc in range(8):
                k.act(sq[kc % 2][:], xt[:, kc, :], AF.Square, [e_xt], [e_sq[kc % 2]])
                k.mm(ps[b][:], cb[:, CB_OD:CB_OD + 128], sq[kc % 2][:], kc == 0, kc == 7, [e_cb, e_sq[kc % 2]], [pe[b]])
        else:
            k.act(sq[:], xt[:], AF.Square, [e_xt], [e_sq])
            for kc in range(8):
                k.mm(ps[b][:], cb[:, CB_OD:CB_OD + 128], sq[:, kc, :], kc == 0, kc == 7, [e_cb, e_sq], [pe[b]])
        k.act(rs[:], ps[b][:], AF.Ln, [e_cf], [pe[b], e_rs], bias=EPS)
        k.act(rs[:], rs[:], AF.Exp, [], [e_rs], scale=-0.5)
        for kc in range(8):
            i2 = kc % 2
            k.stt(tmp[i2][:], xt[:, kc, :], acol(kc), rs[:], ALU.mult, ALU.mult, [e_xt, e_rs, e_dv], [e_tmp[i2]])
            if kc % 2 == 0:
                k.act(dst_fn(kc), tmp[i2][:], AF.Identity, [e_tmp[i2], e_dv], [e_dst], bias=shcol(kc))
            else:
                k.ts("dve", dst_fn(kc), tmp[i2][:], shcol(kc), ALU.add, [e_tmp[i2], e_dv], [e_dst])

    for l in range(L):
        x_src, e_xsrc = (xT_in, Ent()) if l == 0 else (x_s, e_xs)
        x_dst = outT if l == L - 1 else x_s
        P.push()
        hTt = [P.sb([128, 8, TT], BF16) for _ in range(2)]; e_hTt = [Ent(), Ent()]
        xt = [P.sb([128, 8, TT], F32) for _ in range(2)]; e_xt = [Ent(), Ent()]
        sq = P.sb([128, 8, TT], BF16); e_sq = Ent()
        rs = P.sb([128, TT], F32); e_rs = Ent()
        tmp = [P.sb([128, TT], F32) for _ in range(2)]; e_tmp = [Ent(), Ent()]
        k.dma("sp", xt[0][:], fview(x_src, 0, TT), [e_xsrc], [e_xt[0]])
        for tt in range(NT):
            b = tt % 2
            if tt + 1 < NT:
                k.dma("sp", xt[1 - b][:], fview(x_src, (tt + 1) * TT, (tt + 2) * TT), [e_xsrc], [e_xt[1 - b]])
            norm_mod(xt[b], e_xt[b], lambda kc: dv[:, l, 48 + kc:49 + kc], lambda kc: dv[:, l, kc:kc + 1],
                     lambda kc: hTt[b][:, kc, :], e_hTt[b], sq, e_sq, rs, e_rs, tmp, e_tmp)
            k.dma("sp", HT[tt], hTt[b][:], [e_hTt[b]], [e_HT])
        P.pop(); P.barrier()
        if f"hT{l}" in dbg_d:
            dbg_dump(f"hT{l}", HT, e_HT, 8, True)

        P.push()
        wa = P.sb([128, 8, 1024], BF16); e_wa = Ent()
        k.dma("pool", wa[:], wview(w_in[l], C_CA, C_CA + 1024), [], [e_wa])
        dg = P.sb([128, 4, 31, 128], BF16); e_dg2 = [Ent(), Ent()]
        uT = P.sb([128, 4, 30 + S], BF16); e_u = [Ent() for _ in range(NT)]
        e_upad = Ent()
        k.memset("pool", uT[:, :, 0:30], 0.0, [e_upad])
        dg_todo = [(c, j) for c in range(4) for j in range(31)]

        def dg_some(nops):
            for _ in range(nops):
                if dg_todo:
                    c_, j_ = dg_todo.pop(0)
                    k.ts("dve", dg[:, c_, j_, :], cb[:, CB_ID:CB_ID + 128],
                         vec[:, l, 100 + c_ * 31 + j_:101 + c_ * 31 + j_], ALU.mult, [e_cb, e_vec], [e_dg2[0]])
        sgt = [P.sb([128, TT], F32) for _ in range(2)]; e_sgt = [Ent(), Ent()]
        htl = [P.sb([128, 8, TT], BF16) for _ in range(2)]; e_htl = [Ent(), Ent()]
        k.dma("sp", htl[0][:], HT[0], [e_HT], [e_htl[0]])
        for tt in range(NT):
            if tt + 1 < NT:
                k.dma("sp", htl[(tt + 1) % 2][:], HT[tt + 1], [e_HT], [e_htl[(tt + 1) % 2]])
            for c in range(4):
                ba, bg = bank(), bank()
                for kc in range(8):
                    k.mm(ps[ba][:], wa[:, kc, c * 128:(c + 1) * 128], htl[tt % 2][:, kc, :], kc == 0, kc == 7,
                         [e_wa, e_htl[tt % 2]], [pe[ba]])
                for kc in range(8):
                    k.mm(ps[bg][:], wa[:, kc, 512 + c * 128:512 + (c + 1) * 128], htl[tt % 2][:, kc, :], kc == 0, kc == 7,
                         [e_wa, e_htl[tt % 2]], [pe[bg]])
                i2 = c % 2
                k.act(sgt[i2][:], ps[bg][:], AF.Sigmoid, [], [pe[bg], e_sgt[i2]])
                k.tt("dve", uT[:, c, 30 + tt * TT:30 + (tt + 1) * TT], ps[ba][:], sgt[i2][:], ALU.mult, [e_sgt[i2]], [pe[ba], e_u[tt]])
                dg_some((124 + 4 * NT - 1) // (4 * NT))
        dg_some(124)
        defer_mod = (l == 0 and L > 1)
        if defer_mod:
            mw1 = [P.sb([128, 8, 1024], BF16) for _ in range(2)]
            e_mw1 = [Ent(), Ent()]
            mod_dma(1, 0, mw1, e_mw1)
            mod_dma(1, 1, mw1, e_mw1)
        v32 = P.sb([128, 4, TT], F32); e_v32 = Ent()
        vbf = P.sb([128, 4, TT], BF16); e_vbf = Ent()
        vsq = P.sb([128, 4, TT], BF16); e_vsq = Ent()
        mu = P.sb([128, TT], F32); e_mu = Ent()
        var = P.sb([128, TT], F32); e_var = Ent()
        t1 = P.sb([128, 4, TT], F32); e_t1 = Ent()
        oc = [P.sb([128, 4, TT], BF16) for _ in range(2)]; e_oc = [Ent(), Ent()]
        for tt in range(NT):
            for c in range(4):
                b = bank((0, 1, 2, 3, 4, 5, 6))
                rd = [e_dg2[0], e_dg2[1], e_u[tt], e_upad] + ([e_u[tt - 1]] if tt > 0 else [])
                for j in range(31):
                    k.mm(ps[b][:], dg[:, c, j, :], uT[:, c, tt * TT + j:tt * TT + j + TT], j == 0, j == 30, rd, [pe[b]])
                k.act(v32[:, c, :], ps[b][:], AF.Identity, [e_vec], [pe[b], e_v32], bias=vec[:, l, 88 + c:89 + c])
            k.copy("pool", vbf[:], v32[:], [e_v32], [e_vbf])
            k.act(vsq[:], v32[:], AF.Square, [e_v32], [e_vsq])
            bm, bq = bank((0, 1, 2, 3, 4, 5, 6)), bank((0, 1, 2, 3, 4, 5, 6))
            for c in range(4):
                k.mm(ps[bm][:], cb[:, CB_O512:CB_O512 + 128], vbf[:, c, :], c == 0, c == 3, [e_cb, e_vbf], [pe[bm]])
            for c in range(4):
                k.mm(ps[bq][:], cb[:, CB_O512:CB_O512 + 128], vsq[:, c, :], c == 0, c == 3, [e_cb, e_vsq], [pe[bq]])
            k.act(mu[:], ps[bm][:], AF.Identity, [], [pe[bm], e_mu])
            k.stt(var[:], mu[:], -1.0, mu[:], ALU.mult, ALU.mult, [e_mu], [e_var])
            k.tt("dve", var[:], ps[bq][:], var[:], ALU.add, [], [pe[bq], e_var])
            k.ts("dve", var[:], var[:], 0.0, ALU.max, [], [e_var])
            k.act(var[:], var[:], AF.Ln, [e_cf], [e_var], bias=EPS)
            k.act(var[:], var[:], AF.Exp, [], [e_var], scale=-0.5)
            ob = tt % 2
            for c in range(4):
                k.tt("dve", t1[:, c, :], v32[:, c, :], mu[:], ALU.subtract, [e_v32, e_mu], [e_t1])
                k.tt("pool", t1[:, c, :], t1[:, c, :], var[:], ALU.mult, [e_var], [e_t1])
                k.act(oc[ob][:, c, :], t1[:, c, :], AF.Silu, [e_t1, e_vec], [e_oc[ob]],
                      scale=vec[:, l, 92 + c:93 + c], bias=vec[:, l, 96 + c:97 + c])
            k.dma("sp", fview(OC, tt * TT, (tt + 1) * TT), oc[ob][:], [e_oc[ob]], [e_OC])
            if defer_mod:
                grps = list(range(6 * tt // NT, 6 * (tt + 1) // NT))
                for gi, grp in enumerate(grps):
                    mod_group(1, grp, mw1, e_mw1, dma=False)
                    if grp + 2 < 6:
                        mod_dma(1, grp + 2, mw1, e_mw1)
        if defer_mod:
            mod_finish(1)
        P.pop(); P.barrier()
        if f"OC{l}" in dbg_d:
            dbg_dump(f"OC{l}", OC, e_OC, 4, True)
        P.push()
        QT = P.sb([128, 4, S], BF16); KT = P.sb([128, 4, S], BF16)
        VV = P.sb([128, S // 128, 512], BF16)
        e_Q = [[Ent() for _ in range(NT)] for _ in range(4)]
        e_K = [[Ent() for _ in range(NT)] for _ in range(4)]
        e_V = [Ent() for _ in range(S // 128)]
        P.push()
        wqk = P.sb([128, 8, 1024], BF16); e_wqk = Ent()
        wv = P.sb([128, 8, 512], BF16); e_wv = Ent()
        k.dma("pool", wqk[:], wview(w_in[l], C_SQ, C_SQ + 1024), [], [e_wqk])
        k.dma("pool", wv[:], wview(w_in[l], C_SV, C_SV + 512), [], [e_wv])
        sqb = [P.sb([128, TT], BF16) for _ in range(2)]; e_sqb = [Ent(), Ent()]
        q32 = [P.sb([128, TT], F32) for _ in range(2)]; e_q32 = [Ent(), Ent()]
        rsb = [P.sb([128, TT], F32) for _ in range(2)]; e_rsb = [Ent(), Ent()]
        it = 0
        htl = [P.sb([128, 8, TT], BF16) for _ in range(2)]; e_htl = [Ent(), Ent()]
        k.dma("sp", htl[0][:], HT[0], [e_HT], [e_htl[0]])
        for tt in range(NT):
            if tt + 1 < NT:
                k.dma("sp", htl[(tt + 1) % 2][:], HT[tt + 1], [e_HT], [e_htl[(tt + 1) % 2]])
            for which in range(2):
                for c in range(4):
                    i2 = it % 2; it += 1
                    b, b2 = bank(), bank()
                    for kc in range(8):
                        k.mm(ps[b][:], wqk[:, kc, which * 512 + c * 128:which * 512 + (c + 1) * 128],
                             htl[tt % 2][:, kc, :], kc == 0, kc == 7, [e_wqk, e_htl[tt % 2]], [pe[b]])
                    k.act(sqb[i2][:], ps[b][:], AF.Square, [], [pe[b], e_sqb[i2]])
                    k.copy("dve", q32[i2][:], ps[b][:], [], [pe[b], e_q32[i2]])
                    k.mm(ps[b2][:], cb[:, CB_BD64:CB_BD64 + 128], sqb[i2][:], True, True, [e_cb, e_sqb[i2]], [pe[b2]])
                    k.act(rsb[i2][:], ps[b2][:], AF.Ln, [e_cf], [pe[b2], e_rsb[i2]], bias=EPS)
                    k.act(rsb[i2][:], rsb[i2][:], AF.Exp, [], [e_rsb[i2]], scale=-0.5)
                    if which == 0:
                        k.stt(QT[:, c, tt * TT:(tt + 1) * TT], q32[i2][:], dv[:, l, 72:73], rsb[i2][:], ALU.mult, ALU.mult,
                              [e_q32[i2], e_rsb[i2], e_dv], [e_Q[c][tt]])
                    else:
                        k.stt(KT[:, c, tt * TT:(tt + 1) * TT], q32[i2][:], vec[:, l, 234:235], rsb[i2][:], ALU.mult, ALU.mult,
                              [e_q32[i2], e_rsb[i2], e_vec], [e_K[c][tt]])
            for tb in range(tt * 4, tt * 4 + 4):
                b = bank()
                for kc in range(8):
                    k.mm(ps[b][:], htl[tt % 2][:, kc, (tb % 4) * 128:(tb % 4 + 1) * 128], wv[:, kc, :], kc == 0, kc == 7, [e_wv, e_htl[tt % 2]], [pe[b]])
                if tb % 2:
                    k.act(VV[:, tb, :], ps[b][:], AF.Identity, [], [pe[b], e_V[tb]])
                else:
                    k.copy("dve", VV[:, tb, :], ps[b][:], [], [pe[b], e_V[tb]])
        P.pop(); P.barrier()
        nmk = P.sb([128, 2048], BF16); e_nmk = Ent()
        k.dma("sp", nmk[:], cb_d[:, CB_NMASK:CB_NMASK + 2048], [], [e_nmk])
        esb = P.sb([128, 2, TT], F32); e_esb = Ent()
        lkb = [P.sb([128, 2, TT], BF16) for _ in range(2)]; e_lk = [Ent(), Ent()]
        Ls = P.sb([128, 2, TT], F32); e_Ls = Ent()
        Lhi = [P.sb([128, 2, TT], BF16) for _ in range(2)]; e_Lhi = [Ent(), Ent()]
        ATb = [P.sb([128, 2, TT], BF16) for _ in range(2)]; e_AT = [Ent(), Ent()]
        osb = [P.sb([128, TT], BF16) for _ in range(2)]; e_osb = [Ent(), Ent()]
        OB = 6
        wgt = P.sb([128, 8, 3072], BF16); e_wgt = [Ent() for _ in range(3)]
        for q3 in range(3):
            k.dma("pool", wgt[:, :, q3 * 1024:(q3 + 1) * 1024], wview(w_in[l], C_GL + q3 * 1024, C_GL + (q3 + 1) * 1024), [], [e_wgt[q3]])
        hg = [P.sb([128, 8, TT], BF16) for _ in range(2)]; e_hg = [Ent(), Ent()]
        gpre = [P.sb([128, TT], F32) for _ in range(3)]; e_gpre = [Ent() for _ in range(3)]
        gstate = {"n": 0}

        gate_mms = [(tt_, ch_, kc_) for tt_ in range(NT) for ch_ in range(24) for kc_ in range(8)]

        def gate_mm():
            if gstate["n"] >= len(gate_mms):
                return
            n_ = gstate["n"]; gstate["n"] += 1
            tt_, ch_, kc = gate_mms[n_]
            hb = tt_ % 2
            if ch_ == 0 and kc == 0:
                k.dma("sp", hg[hb][:], HT[tt_], [e_HT], [e_hg[hb]])
            k.mm(ps[7][:], wgt[:, kc, ch_ * 128:(ch_ + 1) * 128], hg[hb][:, kc, :], kc == 0, kc == 7,
                 [e_wgt[ch_ // 8], e_hg[hb]], [pe[7]])
            if kc == 7:
                i3_ = (n_ // 8) % 3
                k.copy("dve", gpre[i3_][:], ps[7][:], [], [pe[7], e_gpre[i3_]])
                k.dma("sp", GP[tt_][:, ch_, :], gpre[i3_][:], [e_gpre[i3_]], [e_GP[tt_]])

        def make_group(g, c, base):
            nkb = 4 * (g + 1)
            kbs = list(range(nkb - 1, -1, -1))

            def c0of(i):
                r = kbs[i] - 4 * g
                return 128 * r if r > 0 else 0

            def Zmm(i):
                kb = kbs[i]; r = kb - 4 * g; j3 = (base + i) % 3; c0 = c0of(i)
                for hh in range(2):
                    pb = 64 * hh; zb = 2 * j3 + hh
                    k.mm(ps[zb][:, c0:], KT[pb:pb + 64, c, kb * 128:(kb + 1) * 128], QT[pb:pb + 64, c, g * TT + c0:(g + 1) * TT],
                         True, r < 0, [e_K[c][kb // 4], e_Q[c][g]], [pe[zb]])
                    if r >= 0:
                        k.mm(ps[zb][:, c0:], cb[:, CB_ID:CB_ID + 128], nmk[:, r * 512 + c0:(r + 1) * 512],
                             False, True, [e_cb, e_nmk], [pe[zb]])

            def A_act(i):
                j = (base + i) % 2; j3 = (base + i) % 3; c0 = c0of(i)
                k.act(esb[:, :, c0:], psall[:, 2 * j3:2 * j3 + 2, c0:], AF.Exp, [], [pe[2 * j3], pe[2 * j3 + 1], e_esb])
                k.act(lkb[j][:, :, c0:], esb[:, :, c0:], AF.Ln, [e_esb, e_cf], [e_lk[j]], bias=ONE)

            def tail(i):
                if i == 0:
                    k.memset("pool", Ls[:], 0.0, [e_Ls])
                if i >= nkb - 1:
                    return
                j = (base + i) % 2; jn = (base + i + 1) % 2; c0 = c0of(i); c0n = c0of(i + 1)
                k.tt("dve", Ls[:, :, c0:], Ls[:, :, c0:], lkb[j][:, :, c0:], ALU.add, [e_lk[j]], [e_Ls])
                k.copy("dve", Lhi[jn][:, :, c0n:], Ls[:, :, c0n:], [e_Ls], [e_Lhi[jn]])

            def B_tri(i):
                j = (base + i) % 2; j3 = (base + i) % 3; c0 = c0of(i)
                for hh in range(2):
                    zb = 2 * j3 + hh
                    k.mm(ps[zb][:, c0:], cb[:, CB_NTRI:CB_NTRI + 128], lkb[j][:, hh, c0:], False, i == 0, [e_cb, e_lk[j]], [pe[zb]], skip=True)
                    if i > 0:
                        k.mm(ps[zb][:, c0:], cb[:, CB_NONE:CB_NONE + 128], Lhi[j][:, hh, c0:], False, True, [e_cb, e_Lhi[j]], [pe[zb]], skip=True)

            def B_exp(i):
                j = (base + i) % 2; j3 = (base + i) % 3; c0 = c0of(i)
                k.act(ATb[j][:, :, c0:], psall[:, 2 * j3:2 * j3 + 2, c0:], AF.Exp, [], [pe[2 * j3], pe[2 * j3 + 1], e_AT[j]])

            def AV(i):
                kb = kbs[i]; j = (base + i) % 2; c0 = c0of(i)
                for hh in range(2):
                    h = 2 * c + hh
                    k.mm(ps[OB][64 * hh:64 * hh + 64, c0:], VV[:, kb, h * 64:(h + 1) * 64], ATb[j][:, hh, c0:], i == 0, i == nkb - 1,
                         [e_V[kb], e_AT[j]], [pe[OB]], skip=True)
                if i == nkb - 1:
                    ob_ = (g * 4 + c) % 2
                    k.copy("dve", osb[ob_][:], ps[OB][:], [], [pe[OB], e_osb[ob_]])
                    k.dma("sp", OS[g, :, c, :], osb[ob_][:], [e_osb[ob_]], [e_OS])

            return dict(nkb=nkb, Zmm=Zmm, A_act=A_act, tail=tail, B_tri=B_tri, B_exp=B_exp, AV=AV)

        steps = []
        base = 0
        for g in range(NT):
            for c in range(4):
                G_ = make_group(g, c, base)
                for i in range(G_["nkb"]):
                    steps.append((G_, i))
                base += G_["nkb"]
        G0, i0 = steps[0]
        G0["Zmm"](i0); G0["A_act"](i0); G0["tail"](i0)
        gacc = 0.0
        gper = len(gate_mms) / float(len(steps))
        for kk_ in range(len(steps)):
            Gc, i = steps[kk_]
            nxt = steps[kk_ + 1] if kk_ + 1 < len(steps) else None
            if nxt:
                nxt[0]["Zmm"](nxt[1])
            Gc["B_tri"](i)
            if kk_ >= 1:
                Gp, ip = steps[kk_ - 1]
                Gp["AV"](ip)
            gacc += gper
            while gacc >= 1.0 - 1e-9:
                gate_mm()
                gacc -= 1.0
            if nxt:
                nxt[0]["A_act"](nxt[1])
            Gc["B_exp"](i)
            if nxt:
                nxt[0]["tail"](nxt[1])
        Gl, il = steps[-1]
        Gl["AV"](il)
        while gstate["n"] < len(gate_mms):
            gate_mm()
        P.pop(); P.barrier()
        if f"OS{l}" in dbg_d:
            dbg_dump(f"OS{l}", OS, e_OS, 4, True)
        P.push()
        wh = P.sb([128, 8, 2048], BF16); e_wh = Ent()
        k.dma("pool", wh[:], wview(w_in[l], C_HQ, C_HQ + 2048), [], [e_wh])
        V64 = [P.sb([128, 4, 512], BF16) for _ in range(2)]; e_V64 = [Ent(), Ent()]
        qs = P.sb([128, 4, TT], F32); e_qs = Ent()
        sg = P.sb([128, 4, TT], F32); e_sg = Ent()
        kk = P.sb([128, 4, TT], F32); e_kk = Ent()
        lf = P.sb([128, 4, TT], F32); e_lf = Ent()
        bb = P.sb([128, 4, TT], F32); e_bb = Ent()
        bp = P.sb([128, 4, TT], F32); e_bp = Ent()
        eb = sg; e_eb = e_sg
        enb = lf; e_enb = e_lf
        qt = [P.sb([128, 4, TT], BF16) for _ in range(2)]; e_qt = [Ent(), Ent()]
        kt = P.sb([128, 4, TT], BF16); e_kt = Ent()
        gs = [P.sb([128, 4, TT], BF16) for _ in range(2)]; e_gs = [Ent(), Ent()]
        d1 = [P.sb([128, 4, 8], F32) for _ in range(2)]; d2 = [P.sb([128, 4, 8], F32) for _ in range(2)]
        d3 = [P.sb([128, 4, 8], F32) for _ in range(2)]; e_d = [Ent(), Ent()]
        st32 = P.sb([128, 4, 128], F32); e_st = Ent()
        scA = [P.sb([128, 8, 4, 64], BF16) for _ in range(2)]; e_scA = [[Ent() for _ in range(8)] for _ in range(2)]
        ktok = [P.sb([128, 4, 128], BF16) for _ in range(2)]; e_ktok = [Ent(), Ent()]
        tm2A = [P.sb([128, 8, 4, 128], F32) for _ in range(2)]; e_tm2A = [[Ent() for _ in range(8)] for _ in range(2)]
        stbfA = P.sb([128, 8, 4, 128], BF16); e_stbfA = [Ent() for _ in range(8)]
        oT = P.sb([128, 4, TT], F32); e_oT = Ent()
        ors = P.sb([128, TT], F32); e_ors = Ent()
        ot2 = bb; e_ot2 = e_bb
        oh = P.sb([128, 4, TT], BF16); e_oh = Ent()
        k.memset("pool", st32[:], 0.0, [e_st])
        k.memset("dve", ps[0][:], 0.0, [pe[0]])
        def cmask_bc(po):
            cm = cb[po:po + 64, CB_CMASK:CB_CMASK + 64]
            return bass.AP(cm.tensor, cm.offset, [list(cm.ap[0]), [0, 4], [1, 64]])

        def bc128(d, n):
            v = d[:, :, n:n + 1]
            return bass.AP(v.tensor, v.offset, [list(v.ap[0]), list(v.ap[1]), [0, 128]])

        htl = [P.sb([128, 8, TT], BF16) for _ in range(2)]; e_htl = [Ent(), Ent()]

        def P1(tt):
            p = tt % 2
            hb, e_hb = htl[p], e_htl[p]
            for hh in range(4):
                b = bank((4, 5, 6, 7))
                for kc in range(8):
                    k.mm(ps[b][:], wh[:, kc, hh * 128:(hh + 1) * 128], hb[:, kc, :], kc == 0, kc == 7, [e_wh, e_hb], [pe[b]])
                k.act(qs[:, hh, :], ps[b][:], AF.Silu, [], [pe[b], e_qs])
                b = bank((4, 5, 6, 7))
                for kc in range(8):
                    k.mm(ps[b][:], wh[:, kc, 512 + hh * 128:512 + (hh + 1) * 128], hb[:, kc, :], kc == 0, kc == 7, [e_wh, e_hb], [pe[b]])
                k.act(sg[:, hh, :], ps[b][:], AF.Exp, [], [pe[b], e_sg])
                k.act(lf[:, hh, :], sg[:, hh, :], AF.Ln, [e_sg, e_dv], [e_lf], bias=dv[:, l, 76 + hh:77 + hh])
                k.act(bp[:, hh, :], sg[:, hh, :], AF.Ln, [e_sg, e_cf], [e_bp], bias=ONE)
                k.act(sg[:, hh, :], bp[:, hh, :], AF.Exp, [e_bp], [e_sg], scale=-1.0)
                k.ts("dve", kk[:, hh, :], sg[:, hh, :], dv[:, l, 64 + hh:65 + hh], ALU.mult, [e_sg, e_dv], [e_kk])
                k.tt("pool", lf[:, hh, :], lf[:, hh, :], bp[:, hh, :], ALU.subtract, [e_bp], [e_lf])
                b = bank((4, 5, 6, 7))
                for kc in range(8):
                    k.mm(ps[b][:], wh[:, kc, 1536 + hh * 128:1536 + (hh + 1) * 128], hb[:, kc, :], kc == 0, kc == 7, [e_wh, e_hb], [pe[b]])
                k.act(gs[p][:, hh, :], ps[b][:], AF.Silu, [], [pe[b], e_gs[p]])
            for blk in range(4):
                b = bank((4, 5, 6, 7))
                for kc in range(8):
                    k.mm(ps[b][:], hb[:, kc, blk * 128:(blk + 1) * 128], wh[:, kc, 1024:1536], kc == 0, kc == 7, [e_wh, e_hb], [pe[b]])
                k.copy("dve", V64[p][:, blk, :], ps[b][:], [], [pe[b], e_V64[p]])
            for hh in range(4):
                P.add("dve", (lambda hh: lambda g_: g_.tensor_tensor_scan(out=bb[:, hh, :], data0=cf[:, 0:512], data1=lf[:, hh, :],
                                                                        initial=0.0, op0=ALU.mult, op1=ALU.add))(hh),
                      [e_cf, e_lf], [e_bb])
            bbv = bb[:].rearrange("p h (n c) -> p (h n) c", c=64)
            bpv = bp[:].rearrange("p h (n c) -> p (h n) c", c=64)
            k.tt("dve", bpv, bbv, bcast_last(bbv[:, :, 31:32], 64), ALU.subtract, [e_bb], [e_bp])
            k.act(d1[p][:].rearrange("p h n -> p (h n)"), bbv[:, :, 63], AF.Exp, [e_bb], [e_d[p]])
            k.act(d2[p][:].rearrange("p h n -> p (h n)"), bpv[:, :, 63], AF.Exp, [e_bp], [e_d[p]])
            k.act(d3[p][:].rearrange("p h n -> p (h n)"), bbv[:, :, 31], AF.Exp, [e_bb], [e_d[p]])
            k.act(eb[:], bp[:], AF.Exp, [e_bp], [e_eb])
            k.act(enb[:], bp[:], AF.Exp, [e_bp], [e_enb], scale=-1.0)
            k.tt("dve", qt[p][:], qs[:], eb[:], ALU.mult, [e_qs, e_eb], [e_qt[p]])
            k.tt("pool", kt[:], kk[:], enb[:], ALU.mult, [e_kk, e_enb], [e_kt])

        def P2(tt):
            p = tt % 2
            bS = 0
            for n in range(8):
                csl = slice(n * 64, (n + 1) * 64)
                bK, bU = (1, 2) if n % 2 == 0 else (3, 4)
                po = 64 * (n % 2); blk = n // 2
                for hh in range(4):
                    k.mm(ps[bS][po:po + 64, hh * 64 + 32:(hh + 1) * 64], kt[:, hh, csl], qt[p][:, hh, n * 64 + 32:(n + 1) * 64], True, True,
                         [e_kt, e_qt[p]], [pe[bS]])
                    k.mm(ps[bS][po:po + 32, hh * 64:hh * 64 + 32], kt[:, hh, n * 64:n * 64 + 32], qt[p][:, hh, n * 64:n * 64 + 32], True, True,
                         [e_kt, e_qt[p]], [pe[bS]])
                for hh in range(4):
                    k.mm(ps[bK][po:po + 64, hh * 128:(hh + 1) * 128], kt[:, hh, csl], cb[:, CB_ID:CB_ID + 128], True, True, [e_kt, e_cb], [pe[bK]])
                k.tt("dve", scA[p][po:po + 64, n, :, :], ps[bS][po:po + 64, 0:256].rearrange("p (h c) -> p h c", c=64), cmask_bc(po), ALU.mult,
                     [e_cb], [pe[bS], e_scA[p][n]])
                k.act(ktok[n % 2][po:po + 64], ps[bK][po:po + 64, :].rearrange("p (h c) -> p h c", c=128), AF.Identity, [], [pe[bK], e_ktok[n % 2]])
                for hh in range(4):
                    k.mm(ps[bU][:, hh * 128:(hh + 1) * 128], ktok[n % 2][po:po + 64, hh, :], V64[p][po:po + 64, blk, hh * 128:(hh + 1) * 128], True, True,
                         [e_ktok[n % 2], e_V64[p]], [pe[bU]])
                k.tt("dve", tm2A[p][:, n, :, :], ps[bU][:].rearrange("p (h c) -> p h c", c=128), bc128(d2[p], n), ALU.mult, [e_d[p]],
                     [pe[bU], e_tm2A[p][n]])

        def S3(tt):
            p = tt % 2
            for n in range(8):
                k.tt("dve", stbfA[:, n, :, :], st32[:], bc128(d3[p], n), ALU.mult, [e_st, e_d[p]], [e_stbfA[n]])
                for hh in range(4):
                    k.stt(st32[:, hh, :], st32[:, hh, :], d1[p][:, hh, n:n + 1], tm2A[p][:, n, hh, :], ALU.mult, ALU.add,
                          [e_d[p], e_tm2A[p][n]], [e_st])

        def Q4(tt):
            p = tt % 2
            for n2 in range(4):
                bO = 5 + n2 % 2
                for n in (2 * n2, 2 * n2 + 1):
                    csl = slice(n * 64, (n + 1) * 64)
                    for hh in range(4):
                        oc0 = (n % 2) * 256 + hh * 64
                        po = 64 * (n % 2); blk = n // 2
                        k.mm(ps[bO][:, oc0:oc0 + 64], V64[p][po:po + 64, blk, hh * 128:(hh + 1) * 128], scA[p][po:po + 64, n, hh, :], True, False,
                             [e_V64[p], e_scA[p][n]], [pe[bO]])
                        k.mm(ps[bO][:, oc0:oc0 + 64], stbfA[:, n, hh, :], qt[p][:, hh, csl], False, True, [e_stbfA[n], e_qt[p]], [pe[bO]])
                k.act(oT[:, :, n2 * 128:(n2 + 1) * 128].rearrange("p h (n c) -> p n h c", c=64),
                      ps[bO][:].rearrange("p (n h c) -> p n h c", n=2, h=4), AF.Identity, [], [pe[bO], e_oT])
            osq = qt[p]; e_osq = e_qt[p]
            k.act(osq[:], oT[:], AF.Square, [e_oT], [e_osq])
            for hh in range(4):
                b = bank((4, 5, 6, 7))
                k.mm(ps[b][:], cb[:, CB_O128:CB_O128 + 128], osq[:, hh, :], True, True, [e_cb, e_osq], [pe[b]])
                k.act(ors[:], ps[b][:], AF.Ln, [e_cf], [pe[b], e_ors], bias=EPS)
                k.act(ors[:], ors[:], AF.Exp, [], [e_ors], scale=-0.5)
                k.stt(ot2[:, hh, :], oT[:, hh, :], vec[:, l, 232:233], ors[:], ALU.mult, ALU.mult, [e_oT, e_ors, e_vec], [e_ot2])
                k.tt("pool", oh[:, hh, :], ot2[:, hh, :], gs[p][:, hh, :], ALU.mult, [e_ot2, e_gs[p]], [e_oh])
            k.dma("sp", OH[tt], oh[:], [e_oh], [e_OH])

        k.dma("sp", htl[0][:], HT[0], [e_HT], [e_htl[0]])
        if NT > 1:
            k.dma("sp", htl[1][:], HT[1], [e_HT], [e_htl[1]])
        P1(0)
        P2(0)
        for tt in range(NT):
            S3(tt)
            if tt + 1 < NT:
                P1(tt + 1)
            if tt + 2 < NT:
                k.dma("sp", htl[tt % 2][:], HT[tt + 2], [e_HT], [e_htl[tt % 2]])
            Q4(tt)
            if tt + 1 < NT:
                P2(tt + 1)
        P.pop(); P.barrier()
        if f"OH{l}" in dbg_d:
            dbg_dump(f"OH{l}", OH, e_OH, 4, True)
        P.push()
        wpr = [P.sb([128, 4, D], BF16) for _ in range(3)]; e_wpr = [Ent() for _ in range(3)]
        wo = P.sb([128, 8, D], BF16); e_wo = Ent()
        for q3, wsrc in enumerate((w_cp, w_hp, w_sp)):
            k.dma("pool", wpr[q3][:], wview(wsrc[l], 0, D), [], [e_wpr[q3]])
        k.dma("pool", wo[:], wview(w_out[l], 0, D), [], [e_wo])
        obr = [[P.sb([128, 4, TT], BF16) for _ in range(3)] for _ in range(2)]
        e_obr = [[Ent() for _ in range(3)] for _ in range(2)]
        gpl = [P.sb([128, 3, TT], F32) for _ in range(4)]; e_gpl = [Ent() for _ in range(4)]
        gsb2 = [[P.sb([128, TT], F32) for _ in range(3)] for _ in range(2)]; e_gsb2 = [[Ent() for _ in range(3)] for _ in range(2)]
        xt = [P.sb([128, 8, TT], F32) for _ in range(2)]; e_xt = [Ent(), Ent()]
        mg = [P.sb([128, 8, TT], BF16) for _ in range(2)]; e_mgc = [[Ent() for _ in range(8)] for _ in range(2)]
        ma = [[P.sb([128, TT], F32) for _ in range(3)] for _ in range(2)]; e_ma = [[Ent() for _ in range(3)] for _ in range(2)]
        def loadE(tt_):
            b_ = tt_ % 2
            for i3, (src, esrc) in enumerate(((OC, e_OC), (OH, e_OH), (OS, e_OS))):
                k.dma("sp", obr[b_][i3][:], src[tt_], [esrc], [e_obr[b_][i3]])
            k.dma("sp", xt[b_][:], x_src[tt_], [e_xsrc], [e_xt[b_]])

        gq = [(tt_, c_) for tt_ in range(NT) for c_ in range(8)]

        def loadG(qi):
            if qi < len(gq):
                tt_, c_ = gq[qi]
                k.dma("sp", gpl[qi % 4][:], GP[tt_].rearrange("p (i c) t -> p i c t", c=8)[:, :, c_, :], [e_GP[tt_]], [e_gpl[qi % 4]])

        def Yst(tt):
            b2 = tt % 2
            for c in range(8):
                bs = [bank(), bank(), bank()]
                loadG(tt * 8 + c + 3)
                gsb = gsb2[c % 2]; e_gsb = e_gsb2[c % 2]
                for i3 in range(3):
                    ch = i3 * 8 + c
                    k.act(gsb[i3][:], gpl[(tt * 8 + c) % 4][:, i3, :], AF.Sigmoid, [e_vec, e_gpl[(tt * 8 + c) % 4]], [e_gsb[i3]], bias=vec[:, l, 64 + ch:65 + ch])
                    for kc in range(4):
                        k.mm(ps[bs[i3]][:], wpr[i3][:, kc, c * 128:(c + 1) * 128], obr[b2][i3][:, kc, :], kc == 0, kc == 3,
                             [e_wpr[i3], e_obr[b2][i3]], [pe[bs[i3]]])
                mq = ma[c % 2]; e_mq = e_ma[c % 2]
                for i3 in range(3):
                    k.tt("dve", mq[i3][:], ps[bs[i3]][:], gsb[i3][:], ALU.mult, [e_gsb[i3]], [pe[bs[i3]], e_mq[i3]])
                k.tt("dve", mq[0][:], mq[0][:], mq[1][:], ALU.add, [e_mq[1]], [e_mq[0]])
                k.tt("pool", mg[b2][:, c, :], mq[0][:], mq[2][:], ALU.add, [e_mq[0], e_mq[2]], [e_mgc[b2][c]])

        def Wst(tt):
            b2 = tt % 2
            for c in range(8):
                b = bank()
                for kc in range(8):
                    k.mm(ps[b][:], wo[:, kc, c * 128:(c + 1) * 128], mg[b2][:, kc, :], kc == 0, kc == 7, [e_wo, e_mgc[b2][kc]], [pe[b]])
                k.stt(xt[b2][:, c, :], ps[b][:], dv[:, l, 16 + c:17 + c], xt[b2][:, c, :], ALU.mult, ALU.add, [e_dv], [pe[b], e_xt[b2]])
            k.dma("sp", xm_s[tt], xt[b2][:], [e_xt[b2]], [e_xm])

        loadE(0)
        if NT > 1:
            loadE(1)
        loadG(0); loadG(1); loadG(2)
        Yst(0)
        for tt in range(NT):
            if tt + 1 < NT:
                Yst(tt + 1)
            Wst(tt)
            if tt + 2 < NT:
                loadE(tt + 2)
        P.pop(); P.barrier()
        if f"xm{l}" in dbg_d:
            dbg_dump(f"xm{l}", xm_s, e_xm, 8, False)

        P.push()
        w1 = P.sb([128, 8, 4 * D], BF16); e_w1q = [Ent() for _ in range(4)]
        w2 = P.sb([128, 32, D], BF16); e_w2q = [Ent() for _ in range(4)]
        for q4 in range(4):
            k.dma("pool", w1[:, :, q4 * 1024:(q4 + 1) * 1024], wview(w1_d[l], q4 * 1024, (q4 + 1) * 1024), [], [e_w1q[q4]])
        for q4 in range(4):
            k.dma("pool", w2[:, q4 * 8:(q4 + 1) * 8, :], w2_d[l].rearrange("(kc p) n -> p kc n", p=128)[:, q4 * 8:(q4 + 1) * 8, :], [], [e_w2q[q4]])
        xt = P.sb([128, 8, TT], F32); e_xt = Ent()
        sqr = [P.sb([128, TT], BF16) for _ in range(2)]; e_sqr = [Ent(), Ent()]
        rs = P.sb([128, TT], F32); e_rs = Ent()
        tmp = [P.sb([128, TT], F32) for _ in range(2)]; e_tmp = [Ent(), Ent()]
        h2 = P.sb([128, 8, TT], BF16); e_h2 = Ent()
        hid = P.sb([128, 32, TT], BF16); e_hidj = [Ent() for _ in range(32)]
        rl = P.sb([128, TT], F32); e_rl = Ent()
        xr = [P.sb([128, TT], F32) for _ in range(2)]; e_xr = [Ent(), Ent()]

        def normF():
            norm_mod(xt, e_xt, lambda kc: dv[:, l, 56 + kc:57 + kc], lambda kc: dv[:, l, 24 + kc:25 + kc],
                     lambda kc: h2[:, kc, :], e_h2, sqr, e_sqr, rs, e_rs, tmp, e_tmp)

        k.dma("sp", xt[:], xm_s[0], [e_xm], [e_xt])
        normF()
        if NT > 1:
            k.dma("sp", xt[:], xm_s[1], [e_xm], [e_xt])
        for tt in range(NT):
            for j in range(32):
                b = bank()
                for kc in range(8):
                    k.mm(ps[b][:], w1[:, kc, j * 128:(j + 1) * 128], h2[:, kc, :], kc == 0, kc == 7, [e_w1q[j // 8], e_h2], [pe[b]])
                k.act(rl[:], ps[b][:], AF.Relu, [], [pe[b], e_rl])
                k.tt("pool" if j % 2 else "dve", hid[:, j, :], rl[:], rl[:], ALU.mult, [e_rl], [e_hidj[j]])
            k.dma("sp", xr[0][:], xm_s[tt][:, 0, :], [e_xm], [e_xr[0]])
            if tt + 1 < NT:
                normF()
                if tt + 2 < NT:
                    k.dma("sp", xt[:], xm_s[tt + 2], [e_xm], [e_xt])
            for c in range(8):
                if c + 1 < 8:
                    k.dma("sp", xr[(c + 1) % 2][:], xm_s[tt][:, c + 1, :], [e_xm], [e_xr[(c + 1) % 2]])
                b = bank()
                for j in range(32):
                    k.mm(ps[b][:], w2[:, j, c * 128:(c + 1) * 128], hid[:, j, :], j == 0, j == 31, [e_w2q[j // 8], e_hidj[j]], [pe[b]])
                k.stt(xr[c % 2][:], ps[b][:], dv[:, l, 40 + c:41 + c], xr[c % 2][:], ALU.mult, ALU.add, [e_dv], [pe[b], e_xr[c % 2]])
                k.dma("sp", x_dst[tt][:, c, :], xr[c % 2][:], [e_xr[c % 2]], [e_xs])
        P.pop(); P.barrier()
    P.barrier()
    P.emit()
    global LASTP
    LASTP = P
    return nc


def make_inputs(inp, B0, S, L, shared=None):
    if shared is None:
        cbv, cfv = host_consts()
        shared = {
            "cbf": cbv, "cf32": cfv,
            "vecs": np.ascontiguousarray(np.stack([host_vecs(inp, l) for l in range(L)], axis=1)),
        }
        for n in ("mod_w", "w_in", "w_conv_proj", "w_hgrn_proj", "w_sb_proj", "w_out", "mlp_w1", "mlp_w2"):
            shared[n] = np.ascontiguousarray(np.asarray(inp[n], np.float32)[:L])
    m = dict(shared)
    m["xT"] = np.ascontiguousarray(np.asarray(inp["x"][B0], np.float32)[:S].reshape(S // TT, TT, 8, 128).transpose(0, 3, 2, 1))
    m["cc"] = np.ascontiguousarray(np.asarray(inp["c"][B0], np.float32).reshape(8, 128).T)
    return m


_CACHE = {}


def untile(o):
    nt = o.shape[0]
    return np.ascontiguousarray(o.transpose(0, 3, 2, 1).reshape(nt * TT, 1024))


def kernel(**inputs):
    B, S, _ = inputs["x"].shape
    L = int(np.asarray(inputs["mod_w"]).shape[0])
    key = (S, L)
    if key not in _CACHE:
        _CACHE[key] = build(S, L)
    nc = _CACHE[key]
    inp = {k_: np.asarray(v) for k_, v in inputs.items()}
    first = make_inputs(inp, 0, S, L)
    shared = {k_: v for k_, v in first.items() if k_ not in ("xT", "cc")}
    in_maps = [first] + [make_inputs(inp, b, S, L, shared=shared) for b in range(1, B)]
    res = run_bass_kernel_spmd(nc, in_maps, core_ids=list(range(B)))
    out = np.stack([untile(np.asarray(r["outT"], np.float32)) for r in res.results], axis=0)
    return out.astype(np.float32)
```
